# Optimizing a Trainium2 kernel written in Bass

```python
import math
import jax, jax.numpy as jnp
from jax import lax
import numpy as np

D_MODEL = 2048
BATCH = 1
SEQ = 16384
DEPTH = 4
DEC_BATCH = 16
DEC_SEQ = 16
PAST_LEN = 1024

CHUNK = 64
N_MIXERS = 3
N_A_LAYERS = (DEPTH + N_MIXERS - 1) // N_MIXERS
N_B_LAYERS = (DEPTH - 1 + N_MIXERS - 1) // N_MIXERS
N_C_LAYERS = (DEPTH - 2 + N_MIXERS - 1) // N_MIXERS

N_HEADS = 32
N_KV_HEADS = 4
HEAD_DIM = 64
GROUP = N_HEADS // N_KV_HEADS
QKV_DIM = (N_HEADS + 2 * N_KV_HEADS) * HEAD_DIM
WINDOW = 128
WIN_BLOCKS = WINDOW // CHUNK
ROT_DIM = HEAD_DIM // 4
ROPE_THETA = 500000.0

POOL_WINDOWS = (2, 4, 8, 16)
N_POOL_GROUPS = len(POOL_WINDOWS)
POOL_GROUP_DIM = D_MODEL // N_POOL_GROUPS
POOL_PREFIX = max(POOL_WINDOWS) - 1

CONV_WIDTH = 31
CONV_PREFIX = CONV_WIDTH - 1

D_FF = -(-8 * D_MODEL // (3 * 256)) * 256

RMS_EPS = 1e-5
LN_EPS = 1e-5

kernel_name = "chunk_causal_hybrid_swa_pool_conv_step"


def rms_norm(x, g):
    xf = x.astype(jnp.float32)
    y = xf * lax.rsqrt(jnp.mean(xf * xf, axis=-1, keepdims=True) + RMS_EPS)
    return (y * g.astype(jnp.float32)).astype(x.dtype)


def layer_norm(x, g, b):
    xf = x.astype(jnp.float32)
    mu = jnp.mean(xf, axis=-1, keepdims=True)
    var = jnp.mean(jnp.square(xf - mu), axis=-1, keepdims=True)
    y = (xf - mu) * lax.rsqrt(var + LN_EPS)
    return (y * g.astype(jnp.float32) + b.astype(jnp.float32)).astype(x.dtype)


def rope_partial(x, pos):
    inv_freq = ROPE_THETA ** (-jnp.arange(0, ROT_DIM, 2, dtype=jnp.float32) / ROT_DIM)
    ang = pos.astype(jnp.float32)[:, None] * inv_freq[None, :]
    cos = jnp.cos(ang)[None, :, None, :]
    sin = jnp.sin(ang)[None, :, None, :]
    xr = x[..., :ROT_DIM].astype(jnp.float32)
    x1, x2 = xr[..., :ROT_DIM // 2], xr[..., ROT_DIM // 2:]
    rot = jnp.concatenate([x1 * cos - x2 * sin, x2 * cos + x1 * sin], axis=-1).astype(x.dtype)
    return jnp.concatenate([rot, x[..., ROT_DIM:]], axis=-1)


def qkv_rope(h, w_qkv, b_qkv, pos):
    bsz, seq, _ = h.shape
    qkv = h @ w_qkv + b_qkv
    nq, nk = N_HEADS * HEAD_DIM, N_KV_HEADS * HEAD_DIM
    q = qkv[..., :nq].reshape(bsz, seq, N_HEADS, HEAD_DIM)
    k = qkv[..., nq:nq + nk].reshape(bsz, seq, N_KV_HEADS, HEAD_DIM)
    v = qkv[..., nq + nk:].reshape(bsz, seq, N_KV_HEADS, HEAD_DIM)
    return rope_partial(q, pos), rope_partial(k, pos), v


def sink_softmax(scores, sinks):
    sink = sinks.astype(jnp.float32).reshape(N_KV_HEADS, GROUP, 1, 1)
    m = jnp.maximum(jnp.max(scores, axis=-1, keepdims=True), sink)
    p = jnp.exp(scores - m)
    return p / (jnp.sum(p, axis=-1, keepdims=True) + jnp.exp(sink - m))


def attn_prompt(h, w_qkv, b_qkv, sinks, w_o, b_o):
    bsz, seq, _ = h.shape
    n_ch = seq // CHUNK
    q, k, v = qkv_rope(h, w_qkv, b_qkv, jnp.arange(seq))
    qb = q.reshape(bsz, n_ch, CHUNK, N_KV_HEADS, GROUP, HEAD_DIM)
    pad = ((0, 0), (WIN_BLOCKS, 0), (0, 0), (0, 0), (0, 0))
    kc = jnp.pad(k.reshape(bsz, n_ch, CHUNK, N_KV_HEADS, HEAD_DIM), pad)
    vc = jnp.pad(v.reshape(bsz, n_ch, CHUNK, N_KV_HEADS, HEAD_DIM), pad)
    kb = jnp.concatenate([kc[:, j:j + n_ch] for j in range(WIN_BLOCKS + 1)], axis=2)
    vb = jnp.concatenate([vc[:, j:j + n_ch] for j in range(WIN_BLOCKS + 1)], axis=2)
    scores = jnp.einsum('bnqhgd,bnshd->bnhgqs', qb, kb,
                        preferred_element_type=jnp.float32) * (HEAD_DIM ** -0.5)
    blk = jnp.arange((WIN_BLOCKS + 1) * CHUNK) // CHUNK
    valid = (jnp.arange(n_ch)[:, None] + blk[None, :]) >= WIN_BLOCKS
    scores = jnp.where(valid[None, :, None, None, None, :], scores, -jnp.inf)
    p = sink_softmax(scores, sinks)
    o = jnp.einsum('bnhgqs,bnshd->bnqhgd', p.astype(vb.dtype), vb)
    y = o.reshape(bsz, seq, N_HEADS * HEAD_DIM) @ w_o + b_o
    keep = min(WINDOW, seq)
    return y, k[:, seq - keep:], v[:, seq - keep:]


def attn_sample(h, cache_k, cache_v, w_qkv, b_qkv, sinks, w_o, b_o):
    bsz, seq, _ = h.shape
    w_cache = cache_k.shape[1]
    q, k, v = qkv_rope(h, w_qkv, b_qkv, PAST_LEN + jnp.arange(seq))
    kk = jnp.concatenate([cache_k, k], axis=1)
    vv = jnp.concatenate([cache_v, v], axis=1)
    qg = q.reshape(bsz, seq, N_KV_HEADS, GROUP, HEAD_DIM)
    scores = jnp.einsum('bqhgd,bshd->bhgqs', qg, kk,
                        preferred_element_type=jnp.float32) * (HEAD_DIM ** -0.5)
    p = sink_softmax(scores, sinks)
    o = jnp.einsum('bhgqs,bshd->bqhgd', p.astype(vv.dtype), vv)
    y = o.reshape(bsz, seq, N_HEADS * HEAD_DIM) @ w_o + b_o
    return y, kk[:, -w_cache:], vv[:, -w_cache:]


def pool_mix(h, prefix, pos0, w_group, scale):
    bsz, seq, _ = h.shape
    ext = jnp.concatenate([prefix, h], axis=1)
    cs = jnp.cumsum(jnp.pad(ext.astype(jnp.float32), ((0, 0), (1, 0), (0, 0))), axis=1)
    end = cs[:, POOL_PREFIX + 1:]
    pos = pos0 + jnp.arange(seq)
    outs = []
    for g, w in enumerate(POOL_WINDOWS):
        sl = slice(g * POOL_GROUP_DIM, (g + 1) * POOL_GROUP_DIM)
        start = cs[:, POOL_PREFIX + 1 - w:POOL_PREFIX + 1 - w + seq, sl]
        cnt = jnp.minimum(pos + 1, w).astype(jnp.float32)[None, :, None]
        outs.append((end[..., sl] - start) / cnt - h[..., sl].astype(jnp.float32))
    pooled = jnp.stack(outs, axis=2).astype(h.dtype)
    mixed = jnp.einsum('btgc,gcd->btgd', pooled, w_group).reshape(bsz, seq, D_MODEL)
    return mixed * scale, ext[:, -POOL_PREFIX:]


def conv_mix(h, prefix, w_pw1, b_pw1, w_dw, b_dw, ln_g, ln_b, w_pw2, b_pw2):
    a = h @ w_pw1 + b_pw1
    u = a[..., :D_MODEL] * jax.nn.sigmoid(a[..., D_MODEL:])
    ext = jnp.concatenate([prefix, u], axis=1)
    c = lax.conv_general_dilated(ext, w_dw[:, None, :].astype(ext.dtype), window_strides=(1,),
                                 padding='VALID', dimension_numbers=('NWC', 'WIO', 'NWC'),
                                 feature_group_count=D_MODEL) + b_dw
    z = jax.nn.silu(layer_norm(c, ln_g, ln_b))
    return z @ w_pw2 + b_pw2, ext[:, -CONV_PREFIX:]


def swiglu(h, w_gate_up, w_down):
    a = h @ w_gate_up
    return (jax.nn.silu(a[..., :D_FF]) * a[..., D_FF:]) @ w_down


def setup_inputs(seed: int = 0) -> dict:
    key = jax.random.key(seed)
    ks = jax.random.split(key, 32)
    f32 = jnp.float32

    def nrm(k, shape, scale):
        return jax.random.normal(k, shape, f32) * scale

    w_cache = min(WINDOW, PAST_LEN)
    return {
        'x_prompt': nrm(ks[0], (BATCH, SEQ, D_MODEL), 1.0),
        'x_sample': nrm(ks[1], (DEC_BATCH, DEC_SEQ, D_MODEL), 1.0),
        'cache_k': nrm(ks[2], (N_A_LAYERS, DEC_BATCH, w_cache, N_KV_HEADS, HEAD_DIM), 1.0),
        'cache_v': nrm(ks[3], (N_A_LAYERS, DEC_BATCH, w_cache, N_KV_HEADS, HEAD_DIM), 1.0),
        'state_pool': nrm(ks[4], (N_B_LAYERS, DEC_BATCH, POOL_PREFIX, D_MODEL), 1.0),
        'state_conv': nrm(ks[5], (N_C_LAYERS, DEC_BATCH, CONV_PREFIX, D_MODEL), 0.5),
        'norm_mix': 1.0 + nrm(ks[6], (DEPTH, D_MODEL), 0.02),
        'norm_ffn': 1.0 + nrm(ks[7], (DEPTH, D_MODEL), 0.02),
        'norm_final': 1.0 + nrm(ks[8], (D_MODEL,), 0.02),
        'a_w_qkv': nrm(ks[9], (N_A_LAYERS, D_MODEL, QKV_DIM), D_MODEL ** -0.5),
        'a_b_qkv': nrm(ks[10], (N_A_LAYERS, QKV_DIM), 0.02),
        'a_sinks': nrm(ks[11], (N_A_LAYERS, N_HEADS), 1.0),
        'a_w_o': nrm(ks[12], (N_A_LAYERS, N_HEADS * HEAD_DIM, D_MODEL), (N_HEADS * HEAD_DIM) ** -0.5),
        'a_b_o': nrm(ks[13], (N_A_LAYERS, D_MODEL), 0.02),
        'b_w_group': nrm(ks[14], (N_B_LAYERS, N_POOL_GROUPS, POOL_GROUP_DIM, POOL_GROUP_DIM), POOL_GROUP_DIM ** -0.5),
        'b_scale': 1.0 + nrm(ks[15], (N_B_LAYERS, D_MODEL), 0.02),
        'c_w_pw1': nrm(ks[16], (N_C_LAYERS, D_MODEL, 2 * D_MODEL), D_MODEL ** -0.5),
        'c_b_pw1': nrm(ks[17], (N_C_LAYERS, 2 * D_MODEL), 0.02),
        'c_w_dw': nrm(ks[18], (N_C_LAYERS, CONV_WIDTH, D_MODEL), CONV_WIDTH ** -0.5),
        'c_b_dw': nrm(ks[19], (N_C_LAYERS, D_MODEL), 0.02),
        'c_ln_g': 1.0 + nrm(ks[20], (N_C_LAYERS, D_MODEL), 0.02),
        'c_ln_b': nrm(ks[21], (N_C_LAYERS, D_MODEL), 0.02),
        'c_w_pw2': nrm(ks[22], (N_C_LAYERS, D_MODEL, D_MODEL), D_MODEL ** -0.5),
        'c_b_pw2': nrm(ks[23], (N_C_LAYERS, D_MODEL), 0.02),
        'f_w_gate_up': nrm(ks[24], (DEPTH, D_MODEL, 2 * D_FF), D_MODEL ** -0.5),
        'f_w_down': nrm(ks[25], (DEPTH, D_FF, D_MODEL), D_FF ** -0.5),
    }


def reference(x_prompt, x_sample, cache_k, cache_v, state_pool, state_conv,
              norm_mix, norm_ffn, norm_final,
              a_w_qkv, a_b_qkv, a_sinks, a_w_o, a_b_o,
              b_w_group, b_scale,
              c_w_pw1, c_b_pw1, c_w_dw, c_b_dw, c_ln_g, c_ln_b, c_w_pw2, c_b_pw2,
              f_w_gate_up, f_w_down):
    xp, xs = x_prompt, x_sample
    bp = x_prompt.shape[0]
    kp_l, vp_l, ks_l, vs_l = [], [], [], []
    pp_l, ps_l, cp_l, cs_l = [], [], [], []
    for i in range(DEPTH):
        kind, j = i % N_MIXERS, i // N_MIXERS
        hp = rms_norm(xp, norm_mix[i])
        hs = rms_norm(xs, norm_mix[i])
        if kind == 0:
            yp, nk, nv = attn_prompt(hp, a_w_qkv[j], a_b_qkv[j], a_sinks[j], a_w_o[j], a_b_o[j])
            kp_l.append(nk); vp_l.append(nv)
            ys, nk, nv = attn_sample(hs, cache_k[j], cache_v[j], a_w_qkv[j], a_b_qkv[j],
                                     a_sinks[j], a_w_o[j], a_b_o[j])
            ks_l.append(nk); vs_l.append(nv)
        elif kind == 1:
            zero_prefix = jnp.zeros((bp, POOL_PREFIX, D_MODEL), hp.dtype)
            yp, st = pool_mix(hp, zero_prefix, 0, b_w_group[j], b_scale[j])
            pp_l.append(st)
            ys, st = pool_mix(hs, state_pool[j].astype(hs.dtype), PAST_LEN, b_w_group[j], b_scale[j])
            ps_l.append(st)
        else:
            zero_prefix = jnp.zeros((bp, CONV_PREFIX, D_MODEL), hp.dtype)
            yp, st = conv_mix(hp, zero_prefix, c_w_pw1[j], c_b_pw1[j], c_w_dw[j], c_b_dw[j],
                              c_ln_g[j], c_ln_b[j], c_w_pw2[j], c_b_pw2[j])
            cp_l.append(st)
            ys, st = conv_mix(hs, state_conv[j].astype(hs.dtype), c_w_pw1[j], c_b_pw1[j], c_w_dw[j],
                              c_b_dw[j], c_ln_g[j], c_ln_b[j], c_w_pw2[j], c_b_pw2[j])
            cs_l.append(st)
        xp = xp + yp
        xs = xs + ys
        xp = xp + swiglu(rms_norm(xp, norm_ffn[i]), f_w_gate_up[i], f_w_down[i])
        xs = xs + swiglu(rms_norm(xs, norm_ffn[i]), f_w_gate_up[i], f_w_down[i])
    y_prompt = rms_norm(xp, norm_final)
    y_sample = rms_norm(xs, norm_final)
    return (y_prompt, y_sample,
            jnp.stack(kp_l), jnp.stack(vp_l), jnp.stack(ks_l), jnp.stack(vs_l),
            jnp.stack(pp_l), jnp.stack(ps_l), jnp.stack(cp_l), jnp.stack(cs_l))
```

```python
import contextlib
import numpy as np
import concourse.bass as bass
import concourse.mybir as mybir
from concourse.bass_utils import run_bass_kernel_spmd

F32 = mybir.dt.float32
BF16 = mybir.dt.bfloat16
AF = mybir.ActivationFunctionType
ALU = mybir.AluOpType

NCORES = 8
CHUNK = 64
HALO = 320
ROT = 16
ROPE_THETA = 500000.0
POOL_W = (2, 4, 8, 16)
PPRE = 15
CPRE = 30
CW = 31
EPS = 1e-5
NEG = -30000.0

CFG_FULL = dict(D=2048, NH=32, NKV=4, DFF=5632, FG=4, SEQ=16384, DB=16, DS=16, PAST=1024,
                LAYERS="ABCA", TILES=(8, 8, 7, 7, 7), STILE=2, NSLOT=6)


class SemC:
    def __init__(self, h):
        self.h = h
        self.n = 0


class Buf:
    __slots__ = ("w", "r")

    def __init__(self):
        self.w = None
        self.r = {}


class Eng:
    def __init__(self, name, sem):
        self.name = name
        self.sem = sem
        self.seen = {}
        self.prog = []


class Sched:
    def __init__(self, nc, es, nmisc=10):
        self.nc = nc
        mk = lambda nm: SemC(es.enter_context(nc.semaphore(nm)))
        self.PE = Eng("pe", mk("s_pe"))
        self.ACT = Eng("act", mk("s_act"))
        self.DVE = Eng("dve", mk("s_dve"))
        self.POOL = Eng("pool", mk("s_pool"))
        self.SP = Eng("sp", mk("s_sp"))
        self.misc = [mk(f"s_m{i}") for i in range(nmisc)]
        self.mi = 0
        self.miscp = [mk(f"s_mp{i}") for i in range(8)]
        self.dsems = list(self.misc) + list(self.miscp)
        self._mk = mk

    def newsem(self, nm):
        s = self._mk(nm)
        self.dsems.append(s)
        return s

    def _deps(self, eng, reads, writes, own, skip_own=False):
        deps = {}

        def need(sc, v):
            if deps.get(sc, 0) < v:
                deps[sc] = v
        for b in reads:
            if b.w is not None:
                need(*b.w)
        for b in writes:
            if b.w is not None:
                need(*b.w)
            for sc, v in b.r.items():
                need(sc, v)
        if skip_own:
            deps.pop(own, None)
        waits = [(sc, v) for sc, v in deps.items() if eng.seen.get(sc, 0) < v]
        for sc, v in waits:
            eng.seen[sc] = v
        return waits

    def op(self, eng, fn, reads=(), writes=(), skip_own=False):
        waits = self._deps(eng, reads, writes, eng.sem, skip_own)
        eng.sem.n += 1
        val = eng.sem.n
        eng.prog.append((waits, fn, eng.sem, 1))
        for b in reads:
            b.r[eng.sem] = val
        for b in writes:
            b.w = (eng.sem, val)
            b.r = {}

    def dma(self, q, fn, reads=(), writes=(), sc=None):
        if sc is None:
            lst = self.miscp if q is self.POOL else self.misc
            sc = lst[self.mi % len(lst)]
            self.mi += 1
        waits = self._deps(q, reads, writes, None)
        if sc.n > 0 and q.seen.get(sc, 0) < sc.n:
            waits.append((sc, sc.n))
            q.seen[sc] = sc.n
        sc.n += 16
        val = sc.n
        q.prog.append((waits, fn, sc, 16))
        for b in reads:
            b.r[sc] = val
        for b in writes:
            b.w = (sc, val)
            b.r = {}

    def replay(self, eng, e, final=False):
        for waits, fn, sc, inc in eng.prog:
            for s, v in waits:
                e.wait_ge(s.h, v)
            fn(e).then_inc(sc.h, inc)
        if final:
            for s in self.dsems:
                if s.n > 0:
                    e.wait_ge(s.h, s.n)
            for en in (self.PE, self.ACT, self.DVE, self.POOL):
                if en.sem.n > 0:
                    e.wait_ge(en.sem.h, en.sem.n)


class Plan:
    def __init__(self, cfg):
        self.cfg = cfg
        D = cfg["D"]
        self.D = D
        self.KCD = D // 128
        self.NH, self.NKV = cfg["NH"], cfg["NKV"]
        self.GROUP = self.NH // self.NKV
        assert self.GROUP == 8 and self.NH * 64 == D
        self.NQC = self.NH // 2
        self.DFF = cfg["DFF"]
        self.KCF = self.DFF // 128
        self.FG = cfg["FG"]
        assert self.KCF % self.FG == 0
        self.NGRP = self.KCF // self.FG
        self.OWN = cfg["SEQ"] // NCORES
        self.TP = self.OWN + HALO
        self.NCH = self.TP // CHUNK
        self.TILES = cfg["TILES"]
        assert sum(self.TILES) == self.NCH and self.TILES[0] * CHUNK >= HALO + 16
        self.SPC = cfg["DB"] // NCORES
        self.DS = cfg["DS"]
        self.NSC = self.SPC * self.DS
        self.STILE = cfg["STILE"]
        self.LAYERS = cfg["LAYERS"]
        self.NA = sum(1 for k in self.LAYERS if k == "A")
        self.PG = D // 4 // 128
        self.NTT = max(t * CHUNK + (self.NSC if i == self.STILE else 0) for i, t in enumerate(self.TILES))
        self.NSLOT = cfg["NSLOT"]
        self.vcol = {}
        n = 0

        def add(name, cnt):
            nonlocal n
            self.vcol[name] = n
            n += cnt
        L = len(self.LAYERS)
        for i in range(L):
            add(("nmix", i), self.KCD)
            add(("nffn", i), self.KCD)
        add("nfin", self.KCD)
        for i, k in enumerate(self.LAYERS):
            if k == "A":
                add(("bq", i), self.NQC)
                add(("bqp", i), self.NQC)
                add(("bk", i), self.NKV)
                add(("bkp", i), self.NKV)
                add(("bo", i), self.KCD)
            elif k == "B":
                add(("bscale", i), self.KCD)
            else:
                add(("bpw1a", i), self.KCD)
                add(("bpw1b", i), self.KCD)
                add(("wdw", i), CW * self.KCD)
                add(("bdw", i), self.KCD)
                add(("lng", i), self.KCD)
                add(("lnb", i), self.KCD)
                add(("bpw2", i), self.KCD)
        self.NV = n
        self.wblk = {}
        self.worder = []
        off = 0

        def addw(name, ncols):
            nonlocal off
            self.wblk[name] = (off, ncols)
            self.worder.append(name)
            off += ncols
        for i, k in enumerate(self.LAYERS):
            if k == "A":
                for oc in range(self.NQC):
                    addw(("q", i, oc), D)
                for h in range(self.NKV):
                    addw(("k", i, h), D)
                for half in range(2):
                    addw(("v", i, half), (self.KCD // 2) * self.NKV * 64)
                for oc in range(self.KCD):
                    addw(("o", i, oc), D)
            elif k == "B":
                for g in range(4):
                    for oc in range(self.PG):
                        addw(("pg", i, g, oc), self.PG * 128)
            else:
                for c in range(self.KCD):
                    addw(("pa", i, c), D)
                    addw(("pb", i, c), D)
                for oc in range(self.KCD):
                    addw(("p2", i, oc), D)
            for g in range(self.NGRP):
                for j in range(self.FG):
                    addw(("fg", i, g, j), D)
                    addw(("fu", i, g, j), D)
                for oc in range(self.KCD):
                    addw(("fd", i, g, oc), self.FG * 128)
        self.WTOT = off
        self.SLOTW = max(D, (self.KCD // 2) * self.NKV * 64, self.FG * 128)


def _blk(W, kcs, cols):
    cols = np.asarray(cols)
    parts = [W[kc * 128:(kc + 1) * 128][:, cols] for kc in kcs]
    return np.stack(parts, axis=1).reshape(128, len(kcs) * len(cols))


def _colvec(v, nchunks):
    return np.ascontiguousarray(v.reshape(nchunks, 128).T)


def _rope_perm():
    d = np.arange(64)
    p = d.copy()
    p[:8] = d[:8] + 8
    p[8:16] = d[8:16] - 8
    return p


def prep_shared(P, inp):
    D, KCD = P.D, P.KCD
    wall = np.empty((128, P.WTOT), np.float32)
    vecs = np.zeros((128, P.NV), np.float32)
    perm = _rope_perm()
    allk = list(range(KCD))

    def putw(name, arr):
        o, n = P.wblk[name]
        assert arr.shape == (128, n), (name, arr.shape, n)
        wall[:, o:o + n] = arr

    def putv(name, arr):
        o = P.vcol[name]
        vecs[:, o:o + arr.shape[1]] = arr
    ai = bi = ci = 0
    for i, k in enumerate(P.LAYERS):
        putv(("nmix", i), _colvec(inp["norm_mix"][i], KCD))
        putv(("nffn", i), _colvec(inp["norm_ffn"][i], KCD))
        if k == "A":
            W, b = inp["a_w_qkv"][ai], inp["a_b_qkv"][ai]
            nq = P.NH * 64
            bq = np.zeros((128, P.NQC), np.float32)
            bqp = np.zeros((128, P.NQC), np.float32)
            for oc in range(P.NQC):
                cols = oc * 128 + np.arange(128)
                pc = oc * 128 + np.concatenate([perm, 64 + perm])
                putw(("q", i, oc), _blk(W, allk, cols))
                bq[:, oc] = b[cols]
                bqp[:, oc] = b[pc]
            putv(("bq", i), bq)
            putv(("bqp", i), bqp)
            bk = np.zeros((128, P.NKV), np.float32)
            bkp = np.zeros((128, P.NKV), np.float32)
            for h in range(P.NKV):
                base = nq + h * 64
                cols = base + np.concatenate([np.arange(64), np.arange(64)])
                pc = base + np.concatenate([perm, perm])
                putw(("k", i, h), _blk(W, allk, cols))
                bk[:, h] = b[cols]
                bkp[:, h] = b[pc]
            putv(("bk", i), bk)
            putv(("bkp", i), bkp)
            vcols = nq + P.NKV * 64 + np.arange(P.NKV * 64)
            for half in range(2):
                kcs = list(range(half * KCD // 2, (half + 1) * KCD // 2))
                putw(("v", i, half), _blk(W, kcs, vcols))
            Wo = inp["a_w_o"][ai]
            for oc in range(KCD):
                putw(("o", i, oc), _blk(Wo, allk, oc * 128 + np.arange(128)))
            putv(("bo", i), _colvec(inp["a_b_o"][ai], KCD))
            ai += 1
        elif k == "B":
            Wg = inp["b_w_group"][bi]
            for g in range(4):
                for oc in range(P.PG):
                    putw(("pg", i, g, oc), _blk(Wg[g], list(range(P.PG)), oc * 128 + np.arange(128)))
            putv(("bscale", i), _colvec(inp["b_scale"][bi], KCD))
            bi += 1
        else:
            W1, b1 = inp["c_w_pw1"][ci], inp["c_b_pw1"][ci]
            for c in range(KCD):
                putw(("pa", i, c), _blk(W1, allk, c * 128 + np.arange(128)))
                putw(("pb", i, c), _blk(W1, allk, D + c * 128 + np.arange(128)))
            putv(("bpw1a", i), _colvec(b1[:D], KCD))
            putv(("bpw1b", i), _colvec(b1[D:], KCD))
            wd = inp["c_w_dw"][ci]
            wv = np.zeros((128, CW * KCD), np.float32)
            for c in range(KCD):
                wv[:, c * CW:(c + 1) * CW] = wd[:, c * 128:(c + 1) * 128].T
            putv(("wdw", i), wv)
            putv(("bdw", i), _colvec(inp["c_b_dw"][ci], KCD))
            putv(("lng", i), _colvec(inp["c_ln_g"][ci], KCD))
            putv(("lnb", i), _colvec(inp["c_ln_b"][ci], KCD))
            W2 = inp["c_w_pw2"][ci]
            for oc in range(KCD):
                putw(("p2", i, oc), _blk(W2, allk, oc * 128 + np.arange(128)))
            putv(("bpw2", i), _colvec(inp["c_b_pw2"][ci], KCD))
            ci += 1
        Wgu, Wd = inp["f_w_gate_up"][i], inp["f_w_down"][i]
        for g in range(P.NGRP):
            for j in range(P.FG):
                m = g * P.FG + j
                putw(("fg", i, g, j), _blk(Wgu, allk, m * 128 + np.arange(128)))
                putw(("fu", i, g, j), _blk(Wgu, allk, P.DFF + m * 128 + np.arange(128)))
            kcs = list(range(g * P.FG, (g + 1) * P.FG))
            for oc in range(KCD):
                putw(("fd", i, g, oc), _blk(Wd, kcs, oc * 128 + np.arange(128)))
    putv("nfin", _colvec(inp["norm_final"], KCD))
    return wall, vecs


def prep_core(P, inp, c, shared):
    D, KCD, OWN, TP = P.D, P.KCD, P.OWN, P.TP
    f32 = np.float32
    xp = inp["x_prompt"][0]
    lo = c * OWN - HALO
    seg = np.zeros((TP, D), f32)
    a = max(lo, 0)
    seg[a - lo:] = xp[a:(c + 1) * OWN]
    xs = inp["x_sample"][c * P.SPC:(c + 1) * P.SPC].reshape(P.NSC, D)
    allx = np.concatenate([seg, xs], axis=0)
    xin = np.ascontiguousarray(allx.T.reshape(KCD, 128, TP + P.NSC).transpose(1, 0, 2))
    pos = np.concatenate([lo + np.arange(TP), cfgpast(P) + np.tile(np.arange(P.DS), P.SPC)]).astype(f32)
    inv = (np.float32(ROPE_THETA) ** (-np.arange(0, ROT, 2, dtype=f32) / np.float32(ROT))).astype(f32)
    ang = pos[:, None] * inv[None, :]
    cs, sn = np.cos(ang).astype(f32), np.sin(ang).astype(f32)
    rope = np.zeros((128, 2, TP + P.NSC), f32)
    rope[:, 0, :] = 1.0
    for hh in range(2):
        for d in range(8):
            rope[hh * 64 + d, 0] = cs[:, d]
            rope[hh * 64 + d, 1] = -sn[:, d]
            rope[hh * 64 + 8 + d, 0] = cs[:, d]
            rope[hh * 64 + 8 + d, 1] = sn[:, d]
    valid = (lo + np.arange(TP)) >= 0
    kb = np.full((64, P.NCH + 2), 0.0, f32)
    kb[:, :2] = NEG
    kb[:, 2:] = np.where(valid, 0.0, NEG).astype(f32).reshape(P.NCH, 64).T
    tokmask = np.ascontiguousarray(np.broadcast_to(valid.astype(f32)[None, :], (128, TP)))
    pcorr = np.ones((128, 4, PPRE), f32)
    if c == 0:
        for g, w in enumerate(POOL_W):
            for j in range(PPRE):
                pcorr[:, g, j] = f32(w) / f32(min(j + 1, w))
    NA = max(P.NA, 1)
    vb = np.zeros((NA, 64, P.NKV * 64), f32)
    sinkrow = np.zeros((NA, 1, P.NH * 64), f32)
    nq = P.NH * 64
    for j in range(P.NA):
        vb[j] = inp["a_b_qkv"][j][None, nq + P.NKV * 64:]
        s = inp["a_sinks"][j].reshape(P.NKV, 4, 2)
        sinkrow[j, 0] = np.repeat(s.transpose(0, 2, 1).reshape(-1), 64)
    sl = slice(c * P.SPC, (c + 1) * P.SPC)
    ck = inp["cache_k"][:NA, sl]
    cv = inp["cache_v"][:NA, sl]
    Wc = ck.shape[2]
    assert Wc == 128
    kcT = np.ascontiguousarray(np.concatenate([ck.transpose(0, 4, 1, 3, 2)] * 2, axis=1))
    cvb = cv.reshape(NA, P.SPC, 2, 64, P.NKV, 64).transpose(0, 3, 1, 2, 4, 5)
    vc = np.ascontiguousarray(np.stack([cvb, cvb], axis=5).reshape(NA, 64, P.SPC, 2, P.NKV, 128))
    kc_raw = np.ascontiguousarray(ck.reshape(NA, P.SPC, Wc, P.NKV * 64))
    vc_raw = np.ascontiguousarray(cv.reshape(NA, P.SPC, Wc, P.NKV * 64))
    sp = inp["state_pool"][0, sl]
    spool = np.ascontiguousarray(sp.transpose(2, 0, 1).reshape(KCD, 128, P.SPC, PPRE).transpose(1, 0, 2, 3))
    sc_ = inp["state_conv"][0, sl]
    sconv = np.ascontiguousarray(sc_.transpose(2, 0, 1).reshape(KCD, 128, P.SPC, CPRE).transpose(1, 0, 2, 3))
    wall, vecs = shared
    rperm = np.zeros((128, 128), f32)
    pm = _rope_perm()
    for hh in range(2):
        for d in range(ROT):
            rperm[hh * 64 + pm[d], hh * 64 + d] = 1.0
    return dict(xin=xin, wall=wall, vecs=vecs, rope=rope, kbias=kb, tokmask=tokmask, pcorr=pcorr, vb=vb, ident=np.eye(128, dtype=f32),
                rperm=rperm,
                sinkrow=sinkrow, kcT=kcT, vc=vc, kc_raw=kc_raw, vc_raw=vc_raw, spool=spool, sconv=sconv)


def cfgpast(P):
    return P.cfg["PAST"]


def build(P):
    nc = bass.Bass("TRN2", target_bir_lowering=False)
    D, KCD, NKV, NQC, FG = P.D, P.KCD, P.NKV, P.NQC, P.FG
    TP, NSC, OWN, NTT, NA, SPC, DS = P.TP, P.NSC, P.OWN, P.NTT, max(P.NA, 1), P.SPC, P.DS
    KVW = NKV * 64
    MAXCH = max(P.TILES)

    def din(name, shape):
        return nc.dram_tensor(name, list(shape), F32, kind="ExternalInput").ap()

    def dout(name, shape):
        return nc.dram_tensor(name, list(shape), F32, kind="ExternalOutput").ap()
    xin = din("xin", [128, KCD, TP + NSC])
    wall = din("wall", [128, P.WTOT])
    vecs_d = din("vecs", [128, P.NV])
    rope_d = din("rope", [128, 2, TP + NSC])
    kbias_d = din("kbias", [64, P.NCH + 2])
    tokmask_d = din("tokmask", [128, TP])
    pcorr_d = din("pcorr", [128, 4, PPRE])
    ident_d = din("ident", [128, 128])
    rperm_d = din("rperm", [128, 128])
    vb_d = din("vb", [NA, 64, KVW])
    sinkrow_d = din("sinkrow", [NA, 1, P.NH * 64])
    kcT_d = din("kcT", [NA, 128, SPC, NKV, 128])
    vc_d = din("vc", [NA, 64, SPC, 2, NKV, 128])
    kc_raw = din("kc_raw", [NA, SPC, 128, KVW])
    vc_raw = din("vc_raw", [NA, SPC, 128, KVW])
    spool_d = din("spool", [128, KCD, SPC, PPRE])
    sconv_d = din("sconv", [128, KCD, SPC, CPRE])
    yout = dout("yout", [128, KCD, OWN + NSC])
    kp_out = dout("kp_out", [NA, 64, NKV, 128])
    vp_out = dout("vp_out", [NA, 64, 2, KVW])
    ks_old = dout("ks_old", [NA, SPC, 128 - DS, KVW])
    vs_old = dout("vs_old", [NA, SPC, 128 - DS, KVW])
    ks_new = dout("ks_new", [NA, 64, NKV, NSC])
    vs_new = dout("vs_new", [NA, DS, SPC, KVW])
    pp_out = dout("pp_out", [128, KCD, PPRE])
    ps_out = dout("ps_out", [128, KCD, SPC, PPRE])
    cp_out = dout("cp_out", [128, KCD, CPRE])
    cs_out = dout("cs_out", [128, KCD, SPC, CPRE])

    es = contextlib.ExitStack()
    with es:
        S = Sched(nc, es)
        PE, ACT, DVE, POOL, SP = S.PE, S.ACT, S.DVE, S.POOL, S.SP

        def sb(name, shape, dt=F32):
            return es.enter_context(nc.sbuf_tensor("sb_" + name, list(shape), dt))
        x_t = sb("x", [128, KCD, NTT])
        hA = sb("hA", [128, KCD, NTT], BF16)
        BW = max(KCD, 2 * FG) * NTT
        Bs = sb("Bs", [128, BW], BF16)
        QT = Bs[:, 0:KCD * NTT].rearrange("p (k n) -> p k n", k=KCD)
        MID = Bs[:, 0:2 * FG * NTT].rearrange("p (b j n) -> p b j n", b=2, j=FG)
        KT = sb("KT", [128, 2, NKV, 128 + NTT], BF16)
        Vd = sb("Vd", [64, MAXCH + 2, NKV, 128], BF16)
        KTst = [sb(f"KTst{j}", [128, 2, NKV, 128], BF16) for j in range(NA)]
        Vst = [sb(f"Vst{j}", [64, 2, NKV, 128], BF16) for j in range(NA)]
        NPT = 2
        PT = [sb(f"PT{i}", [64, 3, 512], BF16) for i in range(NPT)]
        ropeT = sb("ropeT", [128, 2, NTT])
        vecs = sb("vecs", [128, P.NV])
        kbias = sb("kbias", [64, P.NCH + 2])
        WS = [sb(f"ws{i}", [128, P.SLOTW], BF16) for i in range(P.NSLOT)]
        wsem = [S.newsem(f"s_w{i}") for i in range(P.NSLOT)]
        NTMP = 5
        TMP = [sb(f"tmp{i}", [128, 512]) for i in range(NTMP)]
        NST = 2 if NTT > 512 else 1
        RSTD = [sb(f"rstd{i}", [128, 512]) for i in range(NST)]
        MU = [sb(f"mu{i}", [128, 512]) for i in range(NST)]
        SEQW = max(CPRE + MAXCH * CHUNK, SPC * (CPRE + DS))
        NSEQ = 8
        SEQ = [sb(f"seq{i}", [128, SEQW]) for i in range(NSEQ)]
        NDG = 4
        DG = [sb(f"dg{i}", [128, 128]) for i in range(NDG)]
        ident_f = sb("ident_f", [128, 128])
        rperm_f = sb("rperm_f", [128, 128])
        s1acc = sb("s1acc", [128, NTT])
        s2acc = sb("s2acc", [128, NTT])
        pstate = sb("pstate", [128, KCD, PPRE])
        cstate = sb("cstate", [128, KCD, CPRE])
        ES_all = sb("ES_all", [128, KCD, SPC, PPRE + DS])
        US_all = sb("US_all", [128, KCD, SPC, CPRE + DS])
        tokm = sb("tokm", [128, P.TILES[0] * CHUNK])
        pcorr = sb("pcorr", [128, 4, PPRE])
        vb = sb("vbt", [64, KVW])
        esrow = sb("esrow", [1, P.NH * 64], BF16)
        ones_bf = sb("ones_bf", [128, 128], BF16)
        ones_f = sb("ones_f", [128, 128])
        KTc = sb("KTc", [128, 2, SPC, NKV, 128], BF16)
        Vc = sb("Vc", [64, SPC, 2, NKV, 128], BF16)
        Vs = sb("Vs", [DS, SPC, NKV, 128], BF16)
        kf32 = sb("kf32", [64, NKV, 128 + NSC])
        RC = sb("rcol", [64, MAXCH + SPC])
        vf32 = sb("vf32", [64, 2 + SPC, KVW])
        PS = [es.enter_context(nc.psum_tensor(f"ps{i}", [128, 512], F32)) for i in range(8)]

        class NS:
            pass
        B = NS()
        B.x = [[Buf() for _ in range(2)] for _ in range(KCD)]
        B.h = [[Buf() for _ in range(2)] for _ in range(KCD)]
        B.q = [[Buf() for _ in range(2)] for _ in range(KCD)]
        B.mid = [[[Buf() for _ in range(2)] for _ in range(FG)] for _ in range(2)]
        B.bs_all = Buf()
        B.kt = Buf(); B.vd = Buf(); B.ktst = [Buf() for _ in range(NA)]; B.vst = [Buf() for _ in range(NA)]
        B.pt = [[Buf() for _ in range(3)] for _ in range(NPT)]
        B.rope = Buf(); B.vecs = Buf(); B.kbias = Buf()
        B.ws = [Buf() for _ in range(P.NSLOT)]
        B.tmp = [Buf() for _ in range(NTMP)]
        B.rstd = [Buf() for _ in range(2)]; B.mu = [Buf() for _ in range(2)]
        B.seq = [Buf() for _ in range(NSEQ)]
        B.dg = [Buf() for _ in range(NDG)]
        B.ident = Buf()
        B.rc = Buf()
        B.rperm = Buf()
        B.s1 = Buf(); B.s2 = Buf(); B.pstate = [Buf() for _ in range(KCD)]; B.cstate = [Buf() for _ in range(KCD)]
        B.es = [Buf() for _ in range(KCD)]; B.us = [Buf() for _ in range(KCD)]
        B.tokm = Buf(); B.pcorr = Buf(); B.vb = Buf(); B.sinkf = Buf(); B.esrow = Buf(); B.ones = Buf()
        B.ktc = Buf(); B.vc = Buf(); B.vs = Buf(); B.kf32 = Buf(); B.vf32 = Buf()
        B.yt = [Buf() for _ in range(2)]
        B.ps = [Buf() for _ in range(8)]
        B.dram = Buf()
        ctr = dict(ps=0, ws=0, tmp=0, seq=0, pt=0, yt=0, dg=0)

        ps_reserved = set()

        def rot(kind, n):
            while True:
                i = ctr[kind] % n
                ctr[kind] += 1
                if kind != "ps" or i not in ps_reserved:
                    return i

        def vcol(name, i=0):
            o = P.vcol[name] + i
            return vecs[:, o:o + 1]

        def wload(name):
            o, n = P.wblk[name]
            s = rot("ws", P.NSLOT)
            S.dma(POOL, lambda e, s=s, o=o, n=n: e.dma_start(out=WS[s][:, 0:n], in_=wall[:, o:o + n]),
                  writes=[B.ws[s]], sc=wsem[s])
            return s

        def linear(units, KCb, act, act_bufs, subt, evac, fine_first=False, hook=None, hook_at=0):
            deferred = []
            PF = max(2, P.NSLOT // max(len(u) for u in units) - 1)
            pending = [[wload(nm) for nm in unit] for unit in units[:PF]]
            for ui, unit in enumerate(units):
                if ui + PF < len(units):
                    pending.append([wload(nm) for nm in units[ui + PF]])
                slots = pending[ui]
                for st, (c0, n) in enumerate(subt):
                    banks = []
                    for bi_, s in enumerate(slots):
                        b = rot("ps", 8)
                        banks.append(b)

                        if fine_first and ui == 0:
                            ab = act_bufs(st)
                            for kc in range(KCb):
                                S.op(PE, lambda e, s=s, b=b, c0=c0, n=n, kc=kc: e.matmul(
                                    PS[b][:, 0:n], WS[s][:, kc * 128:(kc + 1) * 128], act(kc, c0, n),
                                    start=(kc == 0), stop=(kc == KCb - 1)),
                                    reads=[B.ws[s], ab[kc]], writes=[B.ps[b]], skip_own=(kc > 0))
                            continue

                        def mm(e, s=s, b=b, c0=c0, n=n, st=st):
                            ins = None
                            for kc in range(KCb):
                                ins = e.matmul(PS[b][:, 0:n], WS[s][:, kc * 128:(kc + 1) * 128], act(kc, c0, n),
                                               start=(kc == 0), stop=(kc == KCb - 1))
                            return ins
                        S.op(PE, mm, reads=[B.ws[s]] + act_bufs(st), writes=[B.ps[b]])
                    if hook is not None and ui <= hook_at:
                        deferred.append((ui, st, c0, n, banks))
                        if ui == min(hook_at, len(units) - 1) and st == len(subt) - 1:
                            hook()
                            for d_ in deferred:
                                evac(*d_)
                            hook = None
                        continue
                    evac(ui, st, c0, n, banks)

        def tmp():
            i = rot("tmp", NTMP)
            return i

        def rsqrt_inplace(st, n):
            S.op(ACT, lambda e: e.sqrt(out=RSTD[st][:, 0:n], in_=RSTD[st][:, 0:n]), reads=[B.rstd[st]], writes=[B.rstd[st]])
            S.op(DVE, lambda e: e.reciprocal(out=RSTD[st][:, 0:n], in_=RSTD[st][:, 0:n]), reads=[B.rstd[st]], writes=[B.rstd[st]])

        def norm_stats(subt, mask_tile0=False, part=None):
            for st, (c0, n) in enumerate(subt):
                if part == "tail":
                    norm_stats_tail(st, c0, n, mask_tile0)
                    continue
                for kc in range(KCD):
                    ACC, accb = (MU[st], B.mu[st]) if kc % 2 == 0 else (RSTD[st], B.rstd[st])
                    if kc < 2:
                        S.op(ACT, lambda e, c0=c0, n=n, ACC=ACC, kc=kc: e.activation(out=ACC[:, 0:n], in_=x_t[:, kc, c0:c0 + n], func=AF.Square),
                             reads=[B.x[kc][st]], writes=[accb])
                        continue
                    t = tmp()
                    S.op(ACT, lambda e, t=t, kc=kc, c0=c0, n=n: e.activation(out=TMP[t][:, 0:n], in_=x_t[:, kc, c0:c0 + n],
                                                                            func=AF.Square),
                         reads=[B.x[kc][st]], writes=[B.tmp[t]])
                    S.op(DVE, lambda e, t=t, n=n, ACC=ACC: e.tensor_tensor(out=ACC[:, 0:n], in0=ACC[:, 0:n], in1=TMP[t][:, 0:n], op=ALU.add),
                         reads=[B.tmp[t], accb], writes=[accb])
                if KCD > 1:
                    S.op(DVE, lambda e, n=n, st=st: e.tensor_tensor(out=MU[st][:, 0:n], in0=MU[st][:, 0:n], in1=RSTD[st][:, 0:n], op=ALU.add),
                         reads=[B.mu[st], B.rstd[st]], writes=[B.mu[st]])
                if part != "chain":
                    norm_stats_tail(st, c0, n, mask_tile0)

        def norm_stats_tail(st, c0, n, mask_tile0):
            if True:
                b = rot("ps", 8)
                S.op(PE, lambda e, b=b, n=n, st=st: e.matmul(PS[b][:, 0:n], ones_f[:, :], MU[st][:, 0:n], start=True, stop=True),
                     reads=[B.mu[st], B.ones], writes=[B.ps[b]])
                S.op(DVE, lambda e, b=b, st=st, n=n: e.tensor_scalar(out=RSTD[st][:, 0:n], in0=PS[b][:, 0:n],
                                                                    scalar1=1.0 / D, scalar2=EPS, op0=ALU.mult, op1=ALU.add),
                     reads=[B.ps[b]], writes=[B.rstd[st]])
                rsqrt_inplace(st, n)
                if mask_tile0:
                    nm = min(n, max(0, P.TILES[0] * CHUNK - c0))
                    if nm > 0:
                        S.op(DVE, lambda e, st=st, c0=c0, nm=nm: e.tensor_tensor(out=RSTD[st][:, 0:nm], in0=RSTD[st][:, 0:nm],
                                                                              in1=tokm[:, c0:c0 + nm], op=ALU.mult),
                             reads=[B.rstd[st], B.tokm], writes=[B.rstd[st]])

        def norm_to_h(gname, subt):
            norm_stats(subt)
            for st, (c0, n) in enumerate(subt):
                for kc in range(KCD):
                    S.op(DVE, lambda e, kc=kc, st=st, c0=c0, n=n: e.scalar_tensor_tensor(
                        out=hA[:, kc, c0:c0 + n], in0=x_t[:, kc, c0:c0 + n], scalar=vcol(gname, kc),
                        in1=RSTD[st][:, 0:n], op0=ALU.mult, op1=ALU.mult),
                        reads=[B.x[kc][st], B.rstd[st], B.vecs], writes=[B.h[kc][st]])

        hbufs = lambda st: [B.h[kc][st] for kc in range(KCD)]
        hact = lambda kc, c0, n: hA[:, kc, c0:c0 + n]

        def layer_A(li, ti, NP, ncols, subt, has_s, last, subt_out=None, attn_from=0):
            j = sum(1 for k in P.LAYERS[:li] if k == "A")
            DBG = P.cfg.get("ADBG", ())
            if "nos" in DBG:
                has_s = False
            nch = NP // CHUNK
            ch0 = sum(P.TILES[:ti])
            S.dma(SP, lambda e: e.dma_start(out=vb[:, :], in_=vb_d[j]), writes=[B.vb])
            for h_ in range(NKV):
                ts_ = tmp()
                S.dma(SP, lambda e, h_=h_, ts_=ts_: e.dma_start(out=TMP[ts_][0:1, 0:512], in_=sinkrow_d[j][:, h_ * 512:(h_ + 1) * 512]),
                      writes=[B.tmp[ts_]])
                S.op(ACT, lambda e, h_=h_, ts_=ts_: e.activation(out=esrow[0:1, h_ * 512:(h_ + 1) * 512], in_=TMP[ts_][0:1, 0:512],
                                                               func=AF.Exp),
                     reads=[B.tmp[ts_]], writes=[B.esrow])
            for z in range(2):
                S.op(ACT, lambda e, z=z: e.activation(out=KT[:, z, :, 0:128], in_=KTst[j][:, z, :, :], func=AF.Copy),
                     reads=[B.ktst[j]], writes=[B.kt])
            S.op(ACT, lambda e: e.activation(out=Vd[:, 0:2], in_=Vst[j][:, :], func=AF.Copy),
                 reads=[B.vst[j]], writes=[B.vd])
            if has_s:
                for z in range(2):
                    S.dma(POOL, lambda e, z=z: e.dma_start(out=KTc[z * 64:(z + 1) * 64, z], in_=kcT_d[j, z * 64:(z + 1) * 64]),
                          writes=[B.ktc])
                S.dma(POOL, lambda e: e.dma_start(out=Vc[:], in_=vc_d[j]), writes=[B.vc])
                if "nodd" not in DBG:
                    S.dma(SP, lambda e: e.dma_start(out=ks_old[j], in_=kc_raw[j, :, DS:128, :]), reads=[B.dram], writes=[])
                    S.dma(SP, lambda e: e.dma_start(out=vs_old[j], in_=vc_raw[j, :, DS:128, :]), reads=[B.dram], writes=[])
            for st, (c0, n) in enumerate(subt):
                for kc in range(KCD):
                    S.op(ACT, lambda e, kc=kc, c0=c0, n=n: e.mul(out=hA[:, kc, c0:c0 + n], in_=x_t[:, kc, c0:c0 + n],
                                                                mul=vcol(("nmix", li), kc)),
                         reads=[B.x[kc][st], B.vecs], writes=[B.h[kc][st]])
            norm_stats(subt, part="chain")

            RB, RBb = [s1acc, s2acc], [B.s1, B.s2]
            pend = {}

            def rope_finish(ui):
                isq = ui < NQC
                oc = ui if isq else ui - NQC
                r = ui % 2
                for (st, c0, n) in pend.pop(ui):
                    br = rot("ps", 8)
                    S.op(PE, lambda e, br=br, c0=c0, n=n: e.matmul(PS[br][:, 0:n], rperm_f[:, :], RB[r][:, c0:c0 + n], start=True, stop=True),
                         reads=[RBb[r], B.rperm], writes=[B.ps[br]])
                    t1, t2 = tmp(), tmp()
                    S.op(DVE, lambda e, br=br, c0=c0, n=n, t2=t2: e.tensor_tensor(out=TMP[t2][:, 0:n], in0=PS[br][:, 0:n],
                                                                                  in1=ropeT[:, 1, c0:c0 + n], op=ALU.mult),
                         reads=[B.ps[br], B.rope], writes=[B.tmp[t2]])
                    S.op(DVE, lambda e, c0=c0, n=n, t1=t1: e.tensor_tensor(out=TMP[t1][:, 0:n], in0=RB[r][:, c0:c0 + n],
                                                                          in1=ropeT[:, 0, c0:c0 + n], op=ALU.mult),
                         reads=[RBb[r], B.rope], writes=[B.tmp[t1]])
                    rope_out(isq, oc, st, c0, n, t1, t2)

            def evac_qk(ui, st, c0, n, banks):
                isq = ui < NQC
                oc = ui if isq else ui - NQC
                if st == 0 and ui > 0:
                    rope_finish(ui - 1)
                b0 = vcol(("bq", li) if isq else ("bk", li), oc)
                S.op(DVE, lambda e: e.tensor_tensor(out=RB[ui % 2][:, c0:c0 + n], in0=PS[banks[0]][:, 0:n], in1=RSTD[st][:, 0:n],
                                                    op=ALU.mult),
                     reads=[B.ps[banks[0]], B.rstd[st]], writes=[RBb[ui % 2]])
                S.op(ACT, lambda e: e.activation(out=RB[ui % 2][:, c0:c0 + n], in_=RB[ui % 2][:, c0:c0 + n], func=AF.Identity, bias=b0),
                     reads=[RBb[ui % 2], B.vecs], writes=[RBb[ui % 2]])
                pend.setdefault(ui, []).append((st, c0, n))

            def rope_out(isq, oc, st, c0, n, t1, t2):
                if isq:
                    S.op(DVE, lambda e: e.tensor_tensor(out=QT[:, oc, c0:c0 + n], in0=TMP[t1][:, 0:n], in1=TMP[t2][:, 0:n],
                                                        op=ALU.add),
                         reads=[B.tmp[t1], B.tmp[t2]], writes=[B.q[oc][st]])
                else:
                    S.op(DVE, lambda e: e.tensor_tensor(out=TMP[t1][:, 0:n], in0=TMP[t1][:, 0:n], in1=TMP[t2][:, 0:n],
                                                        op=ALU.add),
                         reads=[B.tmp[t1], B.tmp[t2]], writes=[B.tmp[t1]])
                    for z in range(2):
                        S.op(ACT, lambda e, z=z: e.activation(out=KT[z * 64:(z + 1) * 64, z, oc, 128 + c0:128 + c0 + n],
                                                             in_=TMP[t1][z * 64:(z + 1) * 64, 0:n], func=AF.Copy),
                             reads=[B.tmp[t1]], writes=[B.kt])
                    if last:
                        a, bnd = max(c0, NP - 128), min(c0 + n, NP)
                        if bnd > a:
                            S.op(ACT, lambda e, a=a, bnd=bnd: e.activation(out=kf32[:, oc, a - (NP - 128):bnd - (NP - 128)],
                                                             in_=TMP[t1][0:64, a - c0:bnd - c0], func=AF.Copy),
                                 reads=[B.tmp[t1]], writes=[B.kf32])
                    if has_s:
                        a, bnd = max(c0, NP), min(c0 + n, NP + NSC)
                        if bnd > a:
                            S.op(ACT, lambda e, a=a, bnd=bnd: e.activation(out=kf32[:, oc, 128 + a - NP:128 + bnd - NP],
                                                             in_=TMP[t1][0:64, a - c0:bnd - c0], func=AF.Copy),
                                 reads=[B.tmp[t1]], writes=[B.kf32])
            subt_out = subt_out or subt
            in_from = subt[0][0]
            units = [[("q", li, oc)] for oc in range(NQC)] + [[("k", li, h)] for h in range(NKV)]
            linear(units, KCD, hact, hbufs, subt, evac_qk, fine_first=True,
                   hook=lambda: norm_stats(subt, part="tail"), hook_at=2)
            rope_finish(len(units) - 1)

            vs0, vs1 = wload(("v", li, 0)), wload(("v", li, 1))
            HK = KCD // 2
            vgroups = [(c * CHUNK, CHUNK, ("p", c)) for c in range(in_from // CHUNK, nch)]
            if has_s:
                vgroups += [(NP + s * DS, DS, ("s", s)) for s in range(SPC)]
            bcol = rot("ps", 8)
            for gi, (c0, m, kind) in enumerate(vgroups):
                stv = [st for st, (a0, n) in enumerate(subt) if a0 <= c0 < a0 + n][0]
                a0 = subt[stv][0]
                S.op(PE, lambda e, gi=gi, c0=c0, m=m, stv=stv, a0=a0: e.matmul(
                    PS[bcol][0:m, gi:gi + 1], MU[stv][:, c0 - a0:c0 - a0 + m], ones_f[:, 0:1], start=True, stop=True),
                    reads=[B.mu[stv], B.ones], writes=[B.ps[bcol]], skip_own=(gi > 0))
            npg = sum(1 for g_ in vgroups if g_[2][0] == "p")
            for (r0, r1, q0_, q1_) in [(0, 64, 0, npg)] + ([(0, DS, npg, len(vgroups))] if len(vgroups) > npg else []):
                S.op(DVE, lambda e, r1=r1, q0_=q0_, q1_=q1_: e.tensor_scalar(out=RC[0:r1, q0_:q1_], in0=PS[bcol][0:r1, q0_:q1_],
                                                                            scalar1=1.0 / D, scalar2=EPS, op0=ALU.mult, op1=ALU.add),
                     reads=[B.ps[bcol]], writes=[B.rc])
                S.op(ACT, lambda e, r1=r1, q0_=q0_, q1_=q1_: e.sqrt(out=RC[0:r1, q0_:q1_], in_=RC[0:r1, q0_:q1_]),
                     reads=[B.rc], writes=[B.rc])
                S.op(DVE, lambda e, r1=r1, q0_=q0_, q1_=q1_: e.reciprocal(out=RC[0:r1, q0_:q1_], in_=RC[0:r1, q0_:q1_]),
                     reads=[B.rc], writes=[B.rc])
            for gi, (c0, m, kind) in enumerate(vgroups):
                b = rot("ps", 8)

                def mmv(e, b=b, c0=c0, m=m):
                    ins = None
                    for kc in range(KCD):
                        sl_ = vs0 if kc < HK else vs1
                        kk = kc % HK
                        ins = e.matmul(PS[b][0:m, 0:KVW], hA[:, kc, c0:c0 + m], WS[sl_][:, kk * KVW:(kk + 1) * KVW],
                                       start=(kc == 0), stop=(kc == KCD - 1))
                    return ins
                sts = sorted(set(st for st, (a0, n) in enumerate(subt) if a0 < c0 + m and c0 < a0 + n))
                S.op(PE, mmv, reads=[B.ws[vs0], B.ws[vs1]] + [B.h[kc][st] for kc in range(KCD) for st in sts],
                     writes=[B.ps[b]])
                t = tmp()
                S.op(DVE, lambda e, b=b, t=t, m=m, gi=gi: e.scalar_tensor_tensor(out=TMP[t][0:m, 0:KVW], in0=PS[b][0:m, 0:KVW],
                                                                                scalar=RC[0:m, gi:gi + 1], in1=vb[0:m, :],
                                                                                op0=ALU.mult, op1=ALU.add),
                     reads=[B.ps[b], B.vb, B.rc], writes=[B.tmp[t]])
                src3 = lambda t=t, m=m: TMP[t][0:m, 0:KVW].rearrange("p (h d) -> p h d", h=NKV)
                if kind[0] == "p":
                    c = kind[1]
                    for e2 in range(2):
                        S.op(ACT, lambda e, c=c, e2=e2, src3=src3: e.activation(out=Vd[:, 2 + c, :, e2 * 64:(e2 + 1) * 64],
                                                                                in_=src3(), func=AF.Copy),
                             reads=[B.tmp[t]], writes=[B.vd])
                    if last and c >= nch - 2:
                        S.op(ACT, lambda e, c=c, t=t: e.activation(out=vf32[:, c - (nch - 2), :], in_=TMP[t][0:64, 0:KVW],
                                                                   func=AF.Copy),
                             reads=[B.tmp[t]], writes=[B.vf32])
                else:
                    s = kind[1]
                    for e2 in range(2):
                        S.op(ACT, lambda e, s=s, e2=e2, src3=src3: e.activation(out=Vs[:, s, :, e2 * 64:(e2 + 1) * 64],
                                                                                in_=src3(), func=AF.Copy),
                             reads=[B.tmp[t]], writes=[B.vs])
                    S.op(ACT, lambda e, s=s, t=t: e.activation(out=vf32[0:DS, 2 + s, :], in_=TMP[t][0:DS, 0:KVW], func=AF.Copy),
                         reads=[B.tmp[t]], writes=[B.vf32])

            def attn_unit(h, nq, q0, keyblocks, qst):
                W4 = 4 * nq
                pt = rot("pt", NPT)
                sbanks = []
                for kb_i, (ktfn, vap, nk, bias, rb) in enumerate(keyblocks):
                    b = rot("ps", 8)
                    sbanks.append(b)

                    def mms(e, b=b, ktfn=ktfn, nk=nk):
                        ins = None
                        for e2 in P.cfg.get("E2S", (0, 1)):
                            ins = e.matmul(PS[b][0:nk, e2 * W4:(e2 + 1) * W4].rearrange("p (i q) -> p i q", i=4),
                                           ktfn(e2), QT[:, 4 * h:4 * h + 4, q0:q0 + nq],
                                           start=True, stop=True)
                        return ins
                    S.op(PE, mms, reads=rb + [B.q[4 * h + i][qst] for i in range(4)], writes=[B.ps[b]])
                ALVL = P.cfg.get("ALVL", 9)
                if ALVL < 2:
                    return
                for kb_i, (ktfn, vap, nk, bias, rb) in enumerate(keyblocks):
                    b = sbanks[kb_i]
                    if bias is None:
                        fn = lambda e, b=b, nk=nk, kb_i=kb_i: e.activation(out=PT[pt][0:nk, kb_i, 0:2 * W4],
                                                                           in_=PS[b][0:nk, 0:2 * W4], func=AF.Exp, scale=0.125)
                    else:
                        fn = lambda e, b=b, nk=nk, kb_i=kb_i, bias=bias: e.activation(
                            out=PT[pt][0:nk, kb_i, 0:2 * W4], in_=PS[b][0:nk, 0:2 * W4], func=AF.Exp, scale=0.125, bias=bias)
                    S.op(ACT, fn, reads=[B.ps[b], B.kbias], writes=[B.pt[pt][kb_i]])
                if ALVL < 3:
                    return
                bo, bd = rot("ps", 8), rot("ps", 8)
                nkb = len(keyblocks)

                def mmo(e):
                    ins = None
                    for kb_i, (ktfn, vap, nk, bias, rb) in enumerate(keyblocks):
                        ins = e.matmul(PS[bo][:, 0:2 * W4], vap, PT[pt][0:nk, kb_i, 0:2 * W4],
                                       start=(kb_i == 0), stop=(kb_i == nkb - 1))
                    return ins
                S.op(PE, mmo, reads=B.pt[pt][0:nkb] + [B.vd, B.vc, B.vs], writes=[B.ps[bo]])

                if ALVL < 4:
                    return

                def mmd(e):
                    for kb_i, (ktfn, vap, nk, bias, rb) in enumerate(keyblocks):
                        e.matmul(PS[bd][:, 0:2 * W4], ones_bf[0:nk, :], PT[pt][0:nk, kb_i, 0:2 * W4],
                                 start=(kb_i == 0), stop=False)
                    er = esrow[0:1, h * 512:(h + 1) * 512].rearrange("p (g q) -> p g q", g=8)[:, :, 0:nq]
                    return e.matmul(PS[bd][:, 0:2 * W4].rearrange("p (g q) -> p g q", g=8), ones_bf[0:1, :], er,
                                    start=False, stop=True)
                S.op(PE, mmd, reads=B.pt[pt][0:nkb] + [B.ones, B.esrow], writes=[B.ps[bd]])
                if ALVL < 5:
                    return
                t = tmp()
                for e2 in range(2 if ALVL >= 6 else 1):
                    rs, cs_ = slice(e2 * 64, (e2 + 1) * 64), slice(e2 * W4, (e2 + 1) * W4)
                    S.op(DVE, lambda e, rs=rs, cs_=cs_: e.reciprocal(out=TMP[t][rs, cs_], in_=PS[bd][rs, cs_]),
                         reads=[B.ps[bd]], writes=[B.tmp[t]])
                    S.op(DVE, lambda e, rs=rs, cs_=cs_: e.tensor_tensor(
                        out=hA[rs, 4 * h:4 * h + 4, q0:q0 + nq],
                        in0=PS[bo][rs, cs_].rearrange("p (i q) -> p i q", i=4),
                        in1=TMP[t][rs, cs_].rearrange("p (i q) -> p i q", i=4), op=ALU.mult),
                        reads=[B.ps[bo], B.tmp[t]], writes=[B.h[4 * h + i][qst] for i in range(4)])

            def st_of(col):
                for st, (a0, n) in enumerate(subt):
                    if a0 <= col < a0 + n:
                        return st
                raise AssertionError
            for n_ in range(attn_from // CHUNK, nch if "noattn" not in DBG else 0):
                q0 = n_ * CHUNK
                for h in range(NKV):
                    kbs = []
                    for d_ in range(3):
                        cc = n_ + d_
                        kbs.append((lambda e2, cc=cc, h=h: KT[:, e2, h, cc * 64:(cc + 1) * 64],
                                    Vd[:, cc, h, :], 64, kbias[:, ch0 + cc:ch0 + cc + 1], [B.kt]))
                    attn_unit(h, CHUNK, q0, kbs, st_of(q0))
            if has_s and "nosattn" not in DBG:
                for s in range(SPC):
                    q0 = NP + s * DS
                    for h in range(NKV):
                        kbs = []
                        for blk in range(2):
                            kbs.append((lambda e2, s=s, h=h, blk=blk: KTc[:, e2, s, h, blk * 64:(blk + 1) * 64],
                                        Vc[:, s, blk, h, :], 64, None, [B.ktc]))
                        kbs.append((lambda e2, q0=q0, h=h: KT[:, e2, h, 128 + q0:128 + q0 + DS],
                                    Vs[:, s, h, :], DS, None, [B.kt]))
                        attn_unit(h, DS, q0, kbs, st_of(q0))
            for z in range(2):
                S.op(ACT, lambda e, z=z: e.activation(out=KTst[j][:, z, :, :], in_=KT[:, z, :, NP:NP + 128], func=AF.Copy),
                     reads=[B.kt], writes=[B.ktst[j]])
            S.op(ACT, lambda e: e.activation(out=Vst[j][:, :], in_=Vd[:, nch:nch + 2], func=AF.Copy),
                 reads=[B.vd], writes=[B.vst[j]])
            if last:
                S.dma(SP, lambda e: e.dma_start(out=kp_out[j], in_=kf32[:, :, 0:128]), reads=[B.kf32])
                S.dma(SP, lambda e: e.dma_start(out=vp_out[j], in_=vf32[:, 0:2, :]), reads=[B.vf32])
            if has_s:
                S.dma(SP, lambda e: e.dma_start(out=ks_new[j], in_=kf32[:, :, 128:128 + NSC]), reads=[B.kf32])
                S.dma(SP, lambda e: e.dma_start(out=vs_new[j], in_=vf32[0:DS, 2:2 + SPC, :]), reads=[B.vf32])

            def evac_o(ui, st, c0, n, banks):
                S.op(DVE, lambda e: e.scalar_tensor_tensor(out=x_t[:, ui, c0:c0 + n], in0=PS[banks[0]][:, 0:n],
                                                           scalar=vcol(("bo", li), ui), in1=x_t[:, ui, c0:c0 + n],
                                                           op0=ALU.add, op1=ALU.add),
                     reads=[B.ps[banks[0]], B.x[ui][st], B.vecs], writes=[B.x[ui][st]])
            linear([[("o", li, oc)] for oc in range(KCD)], KCD, hact, hbufs, subt_out, evac_o)

        def seq_views(buf, nseg, L):
            return SEQ[buf][:, 0:nseg * L].rearrange("p (s l) -> p s l", s=nseg)

        def layer_B(li, ti, NP, ncols, subt, has_s, last):
            norm_stats(subt, mask_tile0=(ti == 0))
            if has_s:
                for kc in range(KCD):
                    pass
                S.dma(SP, lambda e: e.dma_start(out=ES_all[:, :, :, 0:PPRE], in_=spool_d[:, :, :, :]), writes=B.es)
            for kc in range(KCD):
                g = kc // P.PG
                w = POOL_W[g]
                segs = [("p", 1, NP, 0)]
                if has_s:
                    segs.append(("s", SPC, DS, NP))
                for (kind, nseg, n, col0) in segs:
                    L = PPRE + n
                    if kind == "p":
                        e_i = rot("seq", NSEQ)
                        E = seq_views(e_i, 1, L)
                        ebuf = B.seq[e_i]
                        S.op(ACT, lambda e, E=E, kc=kc: e.activation(out=E[:, 0, 0:PPRE], in_=pstate[:, kc, :], func=AF.Copy),
                             reads=[B.pstate[kc]], writes=[ebuf])
                    else:
                        E = ES_all[:, kc]
                        ebuf = B.es[kc]
                    for st, (a0, nn) in enumerate(subt):
                        lo_, hi_ = max(a0, col0), min(a0 + nn, col0 + nseg * n)
                        if hi_ <= lo_:
                            continue
                        if kind == "p":
                            o_ap = E[:, 0, PPRE + lo_ - col0:PPRE + hi_ - col0]
                            i_ap = x_t[:, kc, lo_:hi_]
                            r_ap = RSTD[st][:, lo_ - a0:hi_ - a0]
                        else:
                            assert lo_ == col0 and hi_ == col0 + nseg * n
                            o_ap = E[:, :, PPRE:PPRE + n]
                            i_ap = x_t[:, kc, lo_:hi_].rearrange("p (s t) -> p s t", s=nseg)
                            r_ap = RSTD[st][:, lo_ - a0:hi_ - a0].rearrange("p (s t) -> p s t", s=nseg)
                        S.op(DVE, lambda e, o_ap=o_ap, i_ap=i_ap, r_ap=r_ap, kc=kc: e.scalar_tensor_tensor(
                            out=o_ap, in0=i_ap, scalar=vcol(("nmix", li), kc), in1=r_ap, op0=ALU.mult, op1=ALU.mult),
                            reads=[B.x[kc][st], B.rstd[st], B.vecs], writes=[ebuf])
                    cur, curb = E, ebuf
                    sh = 1
                    while sh < w:
                        o_i = rot("seq", NSEQ)
                        O = seq_views(o_i, nseg, L)
                        S.op(DVE, lambda e, O=O, cur=cur, sh=sh, L=L: e.tensor_tensor(
                            out=O[:, :, 2 * sh - 1:L], in0=cur[:, :, 2 * sh - 1:L], in1=cur[:, :, sh - 1:L - sh], op=ALU.add),
                            reads=[curb], writes=[B.seq[o_i]])
                        if sh > 1:
                            pass
                        cur, curb = O, B.seq[o_i]
                        sh *= 2
                    if kind == "p" and ti == 0:
                        S.op(DVE, lambda e, cur=cur, g=g: e.tensor_tensor(
                            out=cur[:, 0, PPRE + HALO:PPRE + HALO + PPRE], in0=cur[:, 0, PPRE + HALO:PPRE + HALO + PPRE],
                            in1=pcorr[:, g, :], op=ALU.mult), reads=[curb, B.pcorr], writes=[curb])
                    if kind == "p":
                        o_ap = hA[:, kc, 0:NP].rearrange("p (s t) -> p s t", s=1)
                        hb = [B.h[kc][st] for st in range(len(subt))]
                    else:
                        o_ap = hA[:, kc, col0:col0 + nseg * n].rearrange("p (s t) -> p s t", s=nseg)
                        hb = [B.h[kc][st_] for st_ in range(len(subt))]
                    S.op(DVE, lambda e, o_ap=o_ap, cur=cur, E=E, n=n, w=w: e.scalar_tensor_tensor(
                        out=o_ap, in0=cur[:, :, PPRE:PPRE + n], scalar=1.0 / w, in1=E[:, :, PPRE:PPRE + n],
                        op0=ALU.mult, op1=ALU.subtract), reads=[curb, ebuf], writes=hb)
                    if kind == "p":
                        S.op(ACT, lambda e, E=E, kc=kc, n=n: e.activation(out=pstate[:, kc, :], in_=E[:, 0, n:n + PPRE], func=AF.Copy),
                             reads=[ebuf], writes=[B.pstate[kc]])
            if last:
                S.dma(SP, lambda e: e.dma_start(out=pp_out[:, :, :], in_=pstate[:, :, :]), reads=B.pstate)
            if has_s:
                S.dma(SP, lambda e: e.dma_start(out=ps_out[:, :, :, :], in_=ES_all[:, :, :, DS:DS + PPRE]), reads=B.es)

            def evac_p(ui, st, c0, n, banks):
                oc = ui
                S.op(DVE, lambda e: e.scalar_tensor_tensor(out=x_t[:, oc, c0:c0 + n], in0=PS[banks[0]][:, 0:n],
                                                           scalar=vcol(("bscale", li), oc), in1=x_t[:, oc, c0:c0 + n],
                                                           op0=ALU.mult, op1=ALU.add),
                     reads=[B.ps[banks[0]], B.x[oc][st], B.vecs], writes=[B.x[oc][st]])
            for g in range(4):
                units = [[("pg", li, g, oc)] for oc in range(P.PG)]
                linear(units, P.PG, lambda kc, c0, n, g=g: hA[:, g * P.PG + kc, c0:c0 + n],
                       lambda st, g=g: [B.h[g * P.PG + kc][st] for kc in range(P.PG)], subt,
                       lambda ui, st, c0, n, banks, g=g: evac_p(g * P.PG + ui, st, c0, n, banks), fine_first=True)

        def layer_C(li, ti, NP, ncols, subt, has_s, last):
            norm_to_h(("nmix", li), subt)
            if has_s:
                S.dma(SP, lambda e: e.dma_start(out=US_all[:, :, :, 0:CPRE], in_=sconv_d[:, :, :, :]), writes=B.us)
            state = {}

            def evac_u(ui, st, c0, n, banks):
                c = ui
                if st == 0:
                    u_i = rot("seq", NSEQ)
                    state["u"] = u_i
                    U = seq_views(u_i, 1, CPRE + NP)
                    S.op(ACT, lambda e: e.activation(out=U[:, 0, 0:CPRE], in_=cstate[:, c, :], func=AF.Copy),
                         reads=[B.cstate[c]], writes=[B.seq[u_i]])
                u_i = state["u"]
                U = seq_views(u_i, 1, CPRE + NP)
                t = tmp()
                S.op(ACT, lambda e: e.activation(out=TMP[t][:, 0:n], in_=PS[banks[1]][:, 0:n], func=AF.Sigmoid,
                                                 bias=vcol(("bpw1b", li), c)),
                     reads=[B.ps[banks[1]], B.vecs], writes=[B.tmp[t]])
                lo_, hi_ = c0, min(c0 + n, NP)
                if hi_ > lo_:
                    S.op(DVE, lambda e: e.scalar_tensor_tensor(out=U[:, 0, CPRE + lo_:CPRE + hi_], in0=PS[banks[0]][:, 0:hi_ - lo_],
                                                               scalar=vcol(("bpw1a", li), c), in1=TMP[t][:, 0:hi_ - lo_],
                                                               op0=ALU.add, op1=ALU.mult),
                         reads=[B.ps[banks[0]], B.tmp[t], B.vecs], writes=[B.seq[u_i]])
                    if ti == 0:
                        S.op(DVE, lambda e: e.tensor_tensor(out=U[:, 0, CPRE + lo_:CPRE + hi_], in0=U[:, 0, CPRE + lo_:CPRE + hi_],
                                                            in1=tokm[:, lo_:hi_], op=ALU.mult),
                             reads=[B.seq[u_i], B.tokm], writes=[B.seq[u_i]])
                if has_s and c0 + n > NP:
                    assert c0 <= NP and c0 + n == NP + NSC
                    o_ = NP - c0
                    S.op(DVE, lambda e: e.scalar_tensor_tensor(
                        out=US_all[:, c, :, CPRE:CPRE + DS],
                        in0=PS[banks[0]][:, o_:o_ + NSC].rearrange("p (s t) -> p s t", s=SPC),
                        scalar=vcol(("bpw1a", li), c),
                        in1=TMP[t][:, o_:o_ + NSC].rearrange("p (s t) -> p s t", s=SPC), op0=ALU.add, op1=ALU.mult),
                        reads=[B.ps[banks[0]], B.tmp[t], B.vecs], writes=[B.us[c]])
                if st != len(subt) - 1:
                    return
                segs = [(U, B.seq[u_i], 1, NP, 0)]
                if has_s:
                    segs.append((US_all[:, c], B.us[c], SPC, DS, NP))
                for si_, (X, xb, nseg, n_, col0) in enumerate(segs):
                    npe = P.cfg.get("CONV_NPE", 10) if si_ == 0 else 0
                    nd = CW - npe
                    NDA = 4 if si_ == 0 else 2
                    a_i = [rot("seq", NSEQ) for _ in range(NDA)]
                    A = [seq_views(a, nseg, n_) for a in a_i]
                    bc = None
                    if npe > 0:
                        bc = rot("ps", 8)
                        for jt in range(nd, CW):
                            d_ = rot("dg", NDG)
                            wc = vcol(("wdw", li), c * CW + jt)
                            S.op(ACT, lambda e, d_=d_, wc=wc: e.mul(out=DG[d_][:, :], in_=ident_f[:, :], mul=wc),
                                 reads=[B.ident, B.vecs], writes=[B.dg[d_]])
                            S.op(PE, lambda e, d_=d_, jt=jt, bc=bc, n_=n_, nd=nd: e.matmul(
                                PS[bc][:, 0:n_], DG[d_][:, :], SEQ[u_i][:, jt:jt + n_], start=(jt == nd), stop=(jt == CW - 1)),
                                reads=[B.dg[d_], xb], writes=[B.ps[bc]])
                    for jt in range(nd):
                        k_ = jt % NDA
                        wc = vcol(("wdw", li), c * CW + jt)
                        if jt < NDA:
                            S.op(DVE, lambda e, A=A, k_=k_, X=X, jt=jt, wc=wc, n_=n_: e.tensor_scalar(
                                out=A[k_][:, :, :], in0=X[:, :, jt:jt + n_], scalar1=wc, scalar2=None, op0=ALU.mult),
                                reads=[xb, B.vecs], writes=[B.seq[a_i[k_]]])
                        else:
                            S.op(DVE, lambda e, A=A, k_=k_, X=X, jt=jt, wc=wc, n_=n_: e.scalar_tensor_tensor(
                                out=A[k_][:, :, :], in0=X[:, :, jt:jt + n_], scalar=wc, in1=A[k_][:, :, :],
                                op0=ALU.mult, op1=ALU.add),
                                reads=[xb, B.vecs, B.seq[a_i[k_]]], writes=[B.seq[a_i[k_]]])
                    S.op(DVE, lambda e, A=A: e.scalar_tensor_tensor(out=A[0][:, :, :], in0=A[0][:, :, :],
                                                                    scalar=vcol(("bdw", li), c), in1=A[1][:, :, :],
                                                                    op0=ALU.add, op1=ALU.add),
                         reads=[B.seq[a_i[0]], B.seq[a_i[1]], B.vecs], writes=[B.seq[a_i[0]]])
                    if NDA == 4:
                        S.op(DVE, lambda e, A=A: e.tensor_tensor(out=A[2][:, :, :], in0=A[2][:, :, :], in1=A[3][:, :, :], op=ALU.add),
                             reads=[B.seq[a_i[2]], B.seq[a_i[3]]], writes=[B.seq[a_i[2]]])
                        S.op(DVE, lambda e, A=A: e.tensor_tensor(out=A[0][:, :, :], in0=A[0][:, :, :], in1=A[2][:, :, :], op=ALU.add),
                             reads=[B.seq[a_i[0]], B.seq[a_i[2]]], writes=[B.seq[a_i[0]]])
                    if bc is not None:
                        S.op(DVE, lambda e, A=A, bc=bc, n_=n_: e.tensor_tensor(
                            out=A[0][:, :, :], in0=PS[bc][:, 0:n_].rearrange("p (s t) -> p s t", s=1), in1=A[0][:, :, :], op=ALU.add),
                            reads=[B.seq[a_i[0]], B.ps[bc]], writes=[B.seq[a_i[0]]])
                    if si_ == 0:
                        S.op(ACT, lambda e: e.activation(out=cstate[:, c, :], in_=U[:, 0, NP:NP + CPRE], func=AF.Copy),
                             reads=[B.seq[u_i]], writes=[B.cstate[c]])
                    cv3 = lambda ap2, nseg=nseg: ap2.rearrange("p (s t) -> p s t", s=nseg)
                    qb = [B.q[c][st_] for st_ in range(len(subt))]
                    S.op(ACT, lambda e, A=A, cv3=cv3, col0=col0, nseg=nseg, n_=n_: e.activation(
                        out=cv3(QT[:, c, col0:col0 + nseg * n_]), in_=A[0][:, :, :], func=AF.Copy),
                        reads=[B.seq[a_i[0]]], writes=qb)
                    S.op(ACT, lambda e, A=A: e.activation(out=A[1][:, :, :], in_=A[0][:, :, :], func=AF.Square),
                         reads=[B.seq[a_i[0]]], writes=[B.seq[a_i[1]]])
                    if c == 0:
                        S.op(DVE, lambda e, A=A, cv3=cv3, col0=col0, nseg=nseg, n_=n_: e.tensor_copy(
                            out=cv3(s1acc[:, col0:col0 + nseg * n_]), in_=A[0][:, :, :]), reads=[B.seq[a_i[0]]], writes=[B.s1])
                        S.op(DVE, lambda e, A=A, cv3=cv3, col0=col0, nseg=nseg, n_=n_: e.tensor_copy(
                            out=cv3(s2acc[:, col0:col0 + nseg * n_]), in_=A[1][:, :, :]), reads=[B.seq[a_i[1]]], writes=[B.s2])
                    else:
                        S.op(DVE, lambda e, A=A, cv3=cv3, col0=col0, nseg=nseg, n_=n_: e.tensor_tensor(
                            out=cv3(s1acc[:, col0:col0 + nseg * n_]), in0=cv3(s1acc[:, col0:col0 + nseg * n_]), in1=A[0][:, :, :],
                            op=ALU.add), reads=[B.seq[a_i[0]], B.s1], writes=[B.s1])
                        S.op(DVE, lambda e, A=A, cv3=cv3, col0=col0, nseg=nseg, n_=n_: e.tensor_tensor(
                            out=cv3(s2acc[:, col0:col0 + nseg * n_]), in0=cv3(s2acc[:, col0:col0 + nseg * n_]), in1=A[1][:, :, :],
                            op=ALU.add), reads=[B.seq[a_i[1]], B.s2], writes=[B.s2])
            units = [[("pa", li, c), ("pb", li, c)] for c in range(KCD)]
            linear(units, KCD, hact, hbufs, subt, evac_u, fine_first=True)
            if last:
                S.dma(SP, lambda e: e.dma_start(out=cp_out[:, :, :], in_=cstate[:, :, :]), reads=B.cstate)
            if has_s:
                S.dma(SP, lambda e: e.dma_start(out=cs_out[:, :, :, :], in_=US_all[:, :, :, DS:DS + CPRE]), reads=B.us)
            for st, (c0, n) in enumerate(subt):
                b1, b2 = rot("ps", 8), rot("ps", 8)
                S.op(PE, lambda e, b1=b1, c0=c0, n=n: e.matmul(PS[b1][:, c0:c0 + n], ones_f[:, :], s1acc[:, c0:c0 + n], start=True, stop=True),
                     reads=[B.s1, B.ones], writes=[B.ps[b1]])
                S.op(PE, lambda e, b2=b2, c0=c0, n=n: e.matmul(PS[b2][:, c0:c0 + n], ones_f[:, :], s2acc[:, c0:c0 + n], start=True, stop=True),
                     reads=[B.s2, B.ones], writes=[B.ps[b2]])
                S.op(DVE, lambda e, b1=b1, st=st, n=n, c0=c0: e.tensor_scalar(out=MU[st][:, 0:n], in0=PS[b1][:, c0:c0 + n], scalar1=1.0 / D,
                                                                      scalar2=None, op0=ALU.mult),
                     reads=[B.ps[b1]], writes=[B.mu[st]])
                t = tmp()
                S.op(DVE, lambda e, st=st, n=n, t=t: e.tensor_tensor(out=TMP[t][:, 0:n], in0=MU[st][:, 0:n], in1=MU[st][:, 0:n], op=ALU.mult),
                     reads=[B.mu[st]], writes=[B.tmp[t]])
                S.op(DVE, lambda e, b2=b2, st=st, n=n, t=t, c0=c0: e.scalar_tensor_tensor(out=RSTD[st][:, 0:n], in0=PS[b2][:, c0:c0 + n], scalar=1.0 / D,
                                                                                  in1=TMP[t][:, 0:n], op0=ALU.mult, op1=ALU.subtract),
                     reads=[B.ps[b2], B.tmp[t]], writes=[B.rstd[st]])
                S.op(DVE, lambda e, st=st, n=n: e.tensor_scalar(out=RSTD[st][:, 0:n], in0=RSTD[st][:, 0:n], scalar1=EPS, scalar2=None,
                                                               op0=ALU.add),
                     reads=[B.rstd[st]], writes=[B.rstd[st]])
                rsqrt_inplace(st, n)
                for c in range(KCD):
                    t = tmp()
                    S.op(DVE, lambda e, c=c, c0=c0, n=n, t=t, st=st: e.tensor_tensor(out=TMP[t][:, 0:n], in0=QT[:, c, c0:c0 + n],
                                                                                    in1=MU[st][:, 0:n], op=ALU.subtract),
                         reads=[B.q[c][st], B.mu[st]], writes=[B.tmp[t]])
                    S.op(DVE, lambda e, n=n, t=t, st=st: e.tensor_tensor(out=TMP[t][:, 0:n], in0=TMP[t][:, 0:n], in1=RSTD[st][:, 0:n],
                                                                        op=ALU.mult),
                         reads=[B.tmp[t], B.rstd[st]], writes=[B.tmp[t]])
                    S.op(ACT, lambda e, c=c, c0=c0, n=n, t=t: e.activation(out=hA[:, c, c0:c0 + n], in_=TMP[t][:, 0:n], func=AF.Silu,
                                                                          scale=vcol(("lng", li), c), bias=vcol(("lnb", li), c)),
                         reads=[B.tmp[t], B.vecs], writes=[B.h[c][st]])

            def evac_2(ui, st, c0, n, banks):
                S.op(DVE, lambda e: e.scalar_tensor_tensor(out=x_t[:, ui, c0:c0 + n], in0=PS[banks[0]][:, 0:n],
                                                           scalar=vcol(("bpw2", li), ui), in1=x_t[:, ui, c0:c0 + n],
                                                           op0=ALU.add, op1=ALU.add),
                     reads=[B.ps[banks[0]], B.x[ui][st], B.vecs], writes=[B.x[ui][st]])
            linear([[("p2", li, oc)] for oc in range(KCD)], KCD, hact, hbufs, subt, evac_2, fine_first=True)

        def ffn(li, subt):
            for st, (c0, n) in enumerate(subt):
                for kc in range(KCD):
                    S.op(ACT, lambda e, kc=kc, c0=c0, n=n: e.mul(out=hA[:, kc, c0:c0 + n], in_=x_t[:, kc, c0:c0 + n],
                                                                mul=vcol(("nffn", li), kc)),
                         reads=[B.x[kc][st], B.vecs], writes=[B.h[kc][st]])
            norm_stats(subt, part="chain")
            def gu(g):
                mb = g % 2

                def evac_gu(ui, st, c0, n, banks, mb=mb):
                    t = tmp()
                    S.op(DVE, lambda e: e.tensor_tensor(out=TMP[t][:, 0:n], in0=PS[banks[0]][:, 0:n], in1=RSTD[st][:, 0:n], op=ALU.mult),
                         reads=[B.ps[banks[0]], B.rstd[st]], writes=[B.tmp[t]])
                    S.op(ACT, lambda e: e.activation(out=TMP[t][:, 0:n], in_=TMP[t][:, 0:n], func=AF.Silu),
                         reads=[B.tmp[t]], writes=[B.tmp[t]])
                    S.op(DVE, lambda e: e.tensor_tensor(out=TMP[t][:, 0:n], in0=TMP[t][:, 0:n], in1=RSTD[st][:, 0:n], op=ALU.mult),
                         reads=[B.tmp[t], B.rstd[st]], writes=[B.tmp[t]])
                    S.op(DVE, lambda e: e.tensor_tensor(out=MID[:, mb, ui, c0:c0 + n], in0=PS[banks[1]][:, 0:n],
                                                        in1=TMP[t][:, 0:n], op=ALU.mult),
                         reads=[B.ps[banks[1]], B.tmp[t]], writes=[B.mid[mb][ui][st]])
                units = [[("fg", li, g, j), ("fu", li, g, j)] for j in range(FG)]
                linear(units, KCD, hact, hbufs, subt, evac_gu, fine_first=(g == 0),
                       hook=(lambda: norm_stats(subt, part="tail")) if g == 0 else None, hook_at=2)

            def dn(g):
                mb = g % 2

                def evac_d(ui, st, c0, n, banks):
                    S.op(DVE, lambda e: e.tensor_tensor(out=x_t[:, ui, c0:c0 + n], in0=PS[banks[0]][:, 0:n],
                                                        in1=x_t[:, ui, c0:c0 + n], op=ALU.add),
                         reads=[B.ps[banks[0]], B.x[ui][st]], writes=[B.x[ui][st]])
                linear([[("fd", li, g, oc)] for oc in range(KCD)], FG,
                       lambda kc, c0, n, mb=mb: MID[:, mb, kc, c0:c0 + n],
                       lambda st, mb=mb: [B.mid[mb][kc][st] for kc in range(FG)], subt, evac_d)

            gu(0)
            for g in range(P.NGRP):
                if g + 1 < P.NGRP:
                    gu(g + 1)
                dn(g)

        def alias_guard():
            allb = [b for row in B.q for b in row] + [b for m in B.mid for row in m for b in row]
            S.op(DVE, lambda e: e.memset(TMP[0][0:1, 0:1], 0.0), reads=allb, writes=allb + [B.tmp[0]])

        S.op(DVE, lambda e: e.memset(ones_bf[:, :], 1.0), writes=[B.ones])
        S.op(DVE, lambda e: e.memset(ones_f[:, :], 1.0), writes=[B.ones])
        for j in range(NA):
            for z in range(2):
                S.op(DVE, lambda e, j=j, z=z: e.memset(KTst[j][:, z, :, :], 0.0), writes=[B.ktst[j]])
            S.op(DVE, lambda e, j=j: e.memset(Vst[j][:, :], 0.0), writes=[B.vst[j]])
        for z in range(2):
            S.op(DVE, lambda e, z=z: e.memset(KT[:, z, :, :], 0.0), writes=[B.kt])
            for s_ in range(SPC):
                S.op(DVE, lambda e, z=z, s_=s_: e.memset(KTc[:, z, s_, :, :], 0.0), writes=[B.ktc])
        for i_ in range(NSEQ):
            S.op(DVE, lambda e, i_=i_: e.memset(SEQ[i_][:, :], 0.0), writes=[B.seq[i_]])
        S.op(DVE, lambda e: e.memset(pstate[:, :, :], 0.0), writes=B.pstate)
        S.op(DVE, lambda e: e.memset(cstate[:, :, :], 0.0), writes=B.cstate)
        S.dma(SP, lambda e: e.dma_start(out=vecs[:, :], in_=vecs_d[:, :]), writes=[B.vecs])
        S.dma(SP, lambda e: e.dma_start(out=kbias[:, :], in_=kbias_d[:, :]), writes=[B.kbias])
        S.dma(SP, lambda e: e.dma_start(out=tokm[:, :], in_=tokmask_d[:, 0:P.TILES[0] * CHUNK]), writes=[B.tokm])
        S.dma(SP, lambda e: e.dma_start(out=pcorr[:, :, :], in_=pcorr_d[:, :, :]), writes=[B.pcorr])
        S.dma(SP, lambda e: e.dma_start(out=ident_f[:, :], in_=ident_d[:, :]), writes=[B.ident])
        S.dma(SP, lambda e: e.dma_start(out=rperm_f[:, :], in_=rperm_d[:, :]), writes=[B.rperm])

        t0 = 0
        for ti, tch in enumerate(P.TILES):
            NP = tch * CHUNK
            has_s = (ti == P.STILE)
            last = (ti == len(P.TILES) - 1)
            ncols = NP + (NSC if has_s else 0)
            if ncols > 512:
                h1 = (ncols // 2 + 15) // 16 * 16
                subt = [(0, h1), (h1, ncols - h1)]
            else:
                subt = [(0, ncols)]
            for kc in range(KCD):
                xk = [B.x[kc][st] for st in range(2)]
                S.dma(POOL, lambda e, t0=t0, NP=NP, kc=kc: e.dma_start(out=x_t[:, kc, 0:NP], in_=xin[:, kc, t0:t0 + NP]), writes=xk)
                if has_s:
                    S.dma(POOL, lambda e, NP=NP, kc=kc: e.dma_start(out=x_t[:, kc, NP:NP + NSC], in_=xin[:, kc, TP:TP + NSC]), writes=xk)
            S.dma(SP, lambda e, t0=t0, NP=NP: e.dma_start(out=ropeT[:, :, 0:NP], in_=rope_d[:, :, t0:t0 + NP]), writes=[B.rope])
            if has_s:
                S.dma(SP, lambda e, NP=NP: e.dma_start(out=ropeT[:, :, NP:NP + NSC], in_=rope_d[:, :, TP:TP + NSC]), writes=[B.rope])
            def dump_x():
                for st, (c0, n) in enumerate(subt):
                    for kc in range(KCD):
                        own_lo = max(t0 + c0, HALO)
                        own_hi = min(t0 + c0 + n, t0 + NP)
                        if own_hi > own_lo:
                            S.dma(SP, lambda e, kc=kc, a=own_lo - t0, b=own_hi - t0, o=own_lo - HALO:
                                  e.dma_start(out=yout[:, kc, o:o + (b - a)], in_=x_t[:, kc, a:b]), reads=[B.x[kc][st]])
                        if has_s and c0 + n > NP:
                            a = max(c0, NP)
                            S.dma(SP, lambda e, kc=kc, a=a, b=c0 + n, o=OWN + a - NP:
                                  e.dma_start(out=yout[:, kc, o:o + (b - a)], in_=x_t[:, kc, a:b]), reads=[B.x[kc][st]])
            DUMP = P.cfg.get("DUMPX")
            trim = {}
            if ti == 0 and P.LAYERS == "ABCA" and len(subt) == 1 and not P.cfg.get("NOTRIM"):
                trim = {0: dict(mix=0, attn=128, out=128, ffn=128), 1: dict(mix=128, ffn=128),
                        2: dict(mix=128, ffn=192), 3: dict(mix=192, attn=HALO, out=HALO, ffn=HALO)}
            frm = lambda a: [(a, ncols - a)] if trim else subt
            for li, kind in enumerate(P.LAYERS):
                alias_guard()
                tr = trim.get(li, {})
                if kind == "A":
                    layer_A(li, ti, NP, ncols, frm(tr.get("mix", 0)), has_s, last,
                            subt_out=frm(tr.get("out", 0)), attn_from=tr.get("attn", 0))
                elif kind == "B":
                    layer_B(li, ti, NP, ncols, frm(tr.get("mix", 0)), has_s, last)
                else:
                    layer_C(li, ti, NP, ncols, frm(tr.get("mix", 0)), has_s, last)
                if DUMP == (li, "mix"):
                    dump_x()
                alias_guard()
                ffn(li, frm(tr.get("ffn", 0)))
                if DUMP == (li, "ffn"):
                    dump_x()
            norm_stats(subt)
            for st, (c0, n) in enumerate(subt if DUMP is None else []):
                for kc in range(KCD):
                    yi = tmp()
                    S.op(DVE, lambda e, kc=kc, st=st, c0=c0, n=n, yi=yi: e.scalar_tensor_tensor(
                        out=TMP[yi][:, 0:n], in0=x_t[:, kc, c0:c0 + n], scalar=vcol("nfin", kc),
                        in1=RSTD[st][:, 0:n], op0=ALU.mult, op1=ALU.mult),
                        reads=[B.x[kc][st], B.rstd[st], B.vecs], writes=[B.tmp[yi]])
                    own_lo = max(t0 + c0, HALO)
                    own_hi = min(t0 + c0 + n, t0 + NP)
                    if own_hi > own_lo:
                        S.dma(SP, lambda e, kc=kc, yi=yi, a=own_lo - t0 - c0, b=own_hi - t0 - c0, o=own_lo - HALO:
                              e.dma_start(out=yout[:, kc, o:o + (b - a)], in_=TMP[yi][:, a:b]), reads=[B.tmp[yi]])
                    if has_s and c0 + n > NP:
                        a = max(c0, NP)
                        S.dma(SP, lambda e, kc=kc, yi=yi, a=a - c0, b=n, o=OWN + a - NP:
                              e.dma_start(out=yout[:, kc, o:o + (b - a)], in_=TMP[yi][:, a:b]), reads=[B.tmp[yi]])
            t0 += NP

        with nc.Block() as block:
            @block.tensor
            def _(e):
                S.replay(PE, e)

            @block.scalar
            def _(e):
                S.replay(ACT, e)

            @block.vector
            def _(e):
                S.replay(DVE, e)

            @block.gpsimd
            def _(e):
                S.replay(POOL, e)

            @block.sync
            def _(e):
                S.replay(SP, e, final=True)
    return nc


def assemble(P, res):
    D, KCD, OWN, NA, SPC, DS, NKV = P.D, P.KCD, P.OWN, max(P.NA, 1), P.SPC, P.DS, P.NKV
    f32 = np.float32
    DB = P.cfg["DB"]
    y_p = np.empty((1, P.cfg["SEQ"], D), f32)
    y_s = np.empty((DB, DS, D), f32)
    ks = np.empty((NA, DB, 128, NKV, 64), f32)
    vs = np.empty((NA, DB, 128, NKV, 64), f32)
    ps = np.empty((1, DB, PPRE, D), f32)
    cs = np.empty((1, DB, CPRE, D), f32)

    def fm(a):
        a = np.moveaxis(a, (0, 1), (-1, -2))
        return a.reshape(a.shape[:-2] + (D,))
    for c in range(NCORES):
        r = res[c]
        y = fm(r["yout"])
        y_p[0, c * OWN:(c + 1) * OWN] = y[:OWN]
        y_s[c * SPC:(c + 1) * SPC] = y[OWN:].reshape(SPC, DS, D)
        sl = slice(c * SPC, (c + 1) * SPC)
        ks[:, sl, :128 - DS] = r["ks_old"].reshape(NA, SPC, 128 - DS, NKV, 64)
        vs[:, sl, :128 - DS] = r["vs_old"].reshape(NA, SPC, 128 - DS, NKV, 64)
        kn = r["ks_new"].reshape(NA, 64, NKV, SPC, DS)
        ks[:, sl, 128 - DS:] = kn.transpose(0, 3, 4, 2, 1)
        vn = r["vs_new"].reshape(NA, DS, SPC, NKV, 64)
        vs[:, sl, 128 - DS:] = vn.transpose(0, 2, 1, 3, 4)
        ps[0, sl] = fm(r["ps_out"])
        cs[0, sl] = fm(r["cs_out"])
    r = res[NCORES - 1]
    kp = r["kp_out"].transpose(0, 3, 2, 1)[:, None]
    vp = r["vp_out"].reshape(NA, 64, 2, NKV, 64).transpose(0, 2, 1, 3, 4).reshape(NA, 1, 128, NKV, 64)
    pp = fm(r["pp_out"])[None, None]
    cp = fm(r["cp_out"])[None, None]
    return (y_p, y_s, np.ascontiguousarray(kp), np.ascontiguousarray(vp), ks, vs,
            np.ascontiguousarray(pp), ps, np.ascontiguousarray(cp), cs)


def run(cfg, inputs, trace=False):
    P = Plan(cfg)
    inp = {k: np.asarray(v) for k, v in inputs.items()}
    shared = prep_shared(P, inp)
    in_maps = [prep_core(P, inp, c, shared) for c in range(NCORES)]
    nc = build(P)
    res = run_bass_kernel_spmd(nc, in_maps, core_ids=list(range(NCORES)), **({"trace": True} if trace else {}))
    return assemble(P, res.results), res


def kernel(**inputs):
    outs, _ = run(CFG_FULL, inputs)
    return outs
```

```python
import contextlib
import numpy as np
import concourse.bass as bass
import concourse.mybir as mybir
from concourse.bass_utils import run_bass_kernel_spmd

F32 = mybir.dt.float32
BF16 = mybir.dt.bfloat16
AF = mybir.ActivationFunctionType
ALU = mybir.AluOpType

NCORES = 8
CHUNK = 64
HALO = 320
ROT = 16
ROPE_THETA = 500000.0
POOL_W = (2, 4, 8, 16)
PPRE = 15
CPRE = 30
CW = 31
EPS = 1e-5
NEG = -30000.0

CFG_FULL = dict(D=2048, NH=32, NKV=4, DFF=5632, FG=4, SEQ=16384, DB=16, DS=16, PAST=1024,
                LAYERS="ABCA", TILES=(8, 8, 7, 7, 7), STILE=2, NSLOT=6)


class SemC:
    def __init__(self, h):
        self.h = h
        self.n = 0


class Buf:
    __slots__ = ("w", "r")

    def __init__(self):
        self.w = None
        self.r = {}


class Eng:
    def __init__(self, name, sem):
        self.name = name
        self.sem = sem
        self.seen = {}
        self.prog = []


class Sched:
    def __init__(self, nc, es, nmisc=10):
        self.nc = nc
        mk = lambda nm: SemC(es.enter_context(nc.semaphore(nm)))
        self.PE = Eng("pe", mk("s_pe"))
        self.ACT = Eng("act", mk("s_act"))
        self.DVE = Eng("dve", mk("s_dve"))
        self.POOL = Eng("pool", mk("s_pool"))
        self.SP = Eng("sp", mk("s_sp"))
        self.misc = [mk(f"s_m{i}") for i in range(nmisc)]
        self.mi = 0
        self.miscp = [mk(f"s_mp{i}") for i in range(4)]
        self.dsems = list(self.misc) + list(self.miscp)
        self._mk = mk

    def newsem(self, nm):
        s = self._mk(nm)
        self.dsems.append(s)
        return s

    def _deps(self, eng, reads, writes, own, skip_own=False):
        deps = {}

        def need(sc, v):
            if deps.get(sc, 0) < v:
                deps[sc] = v
        for b in reads:
            if b.w is not None:
                need(*b.w)
        for b in writes:
            if b.w is not None:
                need(*b.w)
            for sc, v in b.r.items():
                need(sc, v)
        if skip_own:
            deps.pop(own, None)
        waits = [(sc, v) for sc, v in deps.items() if eng.seen.get(sc, 0) < v]
        for sc, v in waits:
            eng.seen[sc] = v
        return waits

    def op(self, eng, fn, reads=(), writes=(), skip_own=False):
        waits = self._deps(eng, reads, writes, eng.sem, skip_own)
        eng.sem.n += 1
        val = eng.sem.n
        eng.prog.append((waits, fn, eng.sem, 1))
        for b in reads:
            b.r[eng.sem] = val
        for b in writes:
            b.w = (eng.sem, val)
            b.r = {}

    def dma(self, q, fn, reads=(), writes=(), sc=None):
        if sc is None:
            lst = self.miscp if q is self.POOL else self.misc
            sc = lst[self.mi % len(lst)]
            self.mi += 1
        waits = self._deps(q, reads, writes, None)
        if sc.n > 0 and q.seen.get(sc, 0) < sc.n:
            waits.append((sc, sc.n))
            q.seen[sc] = sc.n
        sc.n += 16
        val = sc.n
        q.prog.append((waits, fn, sc, 16))
        for b in reads:
            b.r[sc] = val
        for b in writes:
            b.w = (sc, val)
            b.r = {}

    def replay(self, eng, e, final=False):
        for waits, fn, sc, inc in eng.prog:
            for s, v in waits:
                e.wait_ge(s.h, v)
            fn(e).then_inc(sc.h, inc)
        if final:
            for s in self.dsems:
                if s.n > 0:
                    e.wait_ge(s.h, s.n)
            for en in (self.PE, self.ACT, self.DVE, self.POOL):
                if en.sem.n > 0:
                    e.wait_ge(en.sem.h, en.sem.n)


class Plan:
    def __init__(self, cfg):
        self.cfg = cfg
        D = cfg["D"]
        self.D = D
        self.KCD = D // 128
        self.NH, self.NKV = cfg["NH"], cfg["NKV"]
        self.GROUP = self.NH // self.NKV
        assert self.GROUP == 8 and self.NH * 64 == D
        self.NQC = self.NH // 2
        self.DFF = cfg["DFF"]
        self.KCF = self.DFF // 128
        self.FG = cfg["FG"]
        assert self.KCF % self.FG == 0
        self.NGRP = self.KCF // self.FG
        self.OWN = cfg["SEQ"] // NCORES
        self.TP = self.OWN + HALO
        self.NCH = self.TP // CHUNK
        self.TILES = cfg["TILES"]
        assert sum(self.TILES) == self.NCH and self.TILES[0] * CHUNK >= HALO + 16
        self.SPC = cfg["DB"] // NCORES
        self.DS = cfg["DS"]
        self.NSC = self.SPC * self.DS
        self.STILE = cfg["STILE"]
        self.LAYERS = cfg["LAYERS"]
        self.NA = sum(1 for k in self.LAYERS if k == "A")
        self.PG = D // 4 // 128
        self.NTT = max(t * CHUNK + (self.NSC if i == self.STILE else 0) for i, t in enumerate(self.TILES))
        self.NSLOT = cfg["NSLOT"]
        self.vcol = {}
        n = 0

        def add(name, cnt):
            nonlocal n
            self.vcol[name] = n
            n += cnt
        L = len(self.LAYERS)
        for i in range(L):
            add(("nmix", i), self.KCD)
            add(("nffn", i), self.KCD)
        add("nfin", self.KCD)
        for i, k in enumerate(self.LAYERS):
            if k == "A":
                add(("bq", i), self.NQC)
                add(("bqp", i), self.NQC)
                add(("bk", i), self.NKV)
                add(("bkp", i), self.NKV)
                add(("bo", i), self.KCD)
            elif k == "B":
                add(("bscale", i), self.KCD)
            else:
                add(("bpw1a", i), self.KCD)
                add(("bpw1b", i), self.KCD)
                add(("wdw", i), CW * self.KCD)
                add(("bdw", i), self.KCD)
                add(("lng", i), self.KCD)
                add(("lnb", i), self.KCD)
                add(("bpw2", i), self.KCD)
        self.NV = n
        self.wblk = {}
        self.worder = []
        off = 0

        def addw(name, ncols):
            nonlocal off
            self.wblk[name] = (off, ncols)
            self.worder.append(name)
            off += ncols
        for i, k in enumerate(self.LAYERS):
            if k == "A":
                for oc in range(self.NQC):
                    addw(("q", i, oc), D)
                for h in range(self.NKV):
                    addw(("k", i, h), D)
                for half in range(2):
                    addw(("v", i, half), (self.KCD // 2) * self.NKV * 64)
                for oc in range(self.KCD):
                    addw(("o", i, oc), D)
            elif k == "B":
                for g in range(4):
                    for oc in range(self.PG):
                        addw(("pg", i, g, oc), self.PG * 128)
            else:
                for c in range(self.KCD):
                    addw(("pa", i, c), D)
                    addw(("pb", i, c), D)
                for oc in range(self.KCD):
                    addw(("p2", i, oc), D)
            for g in range(self.NGRP):
                for j in range(self.FG):
                    addw(("fg", i, g, j), D)
                    addw(("fu", i, g, j), D)
                for oc in range(self.KCD):
                    addw(("fd", i, g, oc), self.FG * 128)
        self.WTOT = off
        self.SLOTW = max(D, (self.KCD // 2) * self.NKV * 64, self.FG * 128)


def _blk(W, kcs, cols):
    cols = np.asarray(cols)
    parts = [W[kc * 128:(kc + 1) * 128][:, cols] for kc in kcs]
    return np.stack(parts, axis=1).reshape(128, len(kcs) * len(cols))


def _colvec(v, nchunks):
    return np.ascontiguousarray(v.reshape(nchunks, 128).T)


def _rope_perm():
    d = np.arange(64)
    p = d.copy()
    p[:8] = d[:8] + 8
    p[8:16] = d[8:16] - 8
    return p


def prep_shared(P, inp):
    D, KCD = P.D, P.KCD
    wall = np.empty((128, P.WTOT), np.float32)
    vecs = np.zeros((128, P.NV), np.float32)
    perm = _rope_perm()
    allk = list(range(KCD))

    def putw(name, arr):
        o, n = P.wblk[name]
        assert arr.shape == (128, n), (name, arr.shape, n)
        wall[:, o:o + n] = arr

    def putv(name, arr):
        o = P.vcol[name]
        vecs[:, o:o + arr.shape[1]] = arr
    ai = bi = ci = 0
    for i, k in enumerate(P.LAYERS):
        putv(("nmix", i), _colvec(inp["norm_mix"][i], KCD))
        putv(("nffn", i), _colvec(inp["norm_ffn"][i], KCD))
        if k == "A":
            W, b = inp["a_w_qkv"][ai], inp["a_b_qkv"][ai]
            nq = P.NH * 64
            bq = np.zeros((128, P.NQC), np.float32)
            bqp = np.zeros((128, P.NQC), np.float32)
            for oc in range(P.NQC):
                cols = oc * 128 + np.arange(128)
                pc = oc * 128 + np.concatenate([perm, 64 + perm])
                putw(("q", i, oc), _blk(W, allk, cols))
                bq[:, oc] = b[cols]
                bqp[:, oc] = b[pc]
            putv(("bq", i), bq)
            putv(("bqp", i), bqp)
            bk = np.zeros((128, P.NKV), np.float32)
            bkp = np.zeros((128, P.NKV), np.float32)
            for h in range(P.NKV):
                base = nq + h * 64
                cols = base + np.concatenate([np.arange(64), np.arange(64)])
                pc = base + np.concatenate([perm, perm])
                putw(("k", i, h), _blk(W, allk, cols))
                bk[:, h] = b[cols]
                bkp[:, h] = b[pc]
            putv(("bk", i), bk)
            putv(("bkp", i), bkp)
            vcols = nq + P.NKV * 64 + np.arange(P.NKV * 64)
            for half in range(2):
                kcs = list(range(half * KCD // 2, (half + 1) * KCD // 2))
                putw(("v", i, half), _blk(W, kcs, vcols))
            Wo = inp["a_w_o"][ai]
            for oc in range(KCD):
                putw(("o", i, oc), _blk(Wo, allk, oc * 128 + np.arange(128)))
            putv(("bo", i), _colvec(inp["a_b_o"][ai], KCD))
            ai += 1
        elif k == "B":
            Wg = inp["b_w_group"][bi]
            for g in range(4):
                for oc in range(P.PG):
                    putw(("pg", i, g, oc), _blk(Wg[g], list(range(P.PG)), oc * 128 + np.arange(128)))
            putv(("bscale", i), _colvec(inp["b_scale"][bi], KCD))
            bi += 1
        else:
            W1, b1 = inp["c_w_pw1"][ci], inp["c_b_pw1"][ci]
            for c in range(KCD):
                putw(("pa", i, c), _blk(W1, allk, c * 128 + np.arange(128)))
                putw(("pb", i, c), _blk(W1, allk, D + c * 128 + np.arange(128)))
            putv(("bpw1a", i), _colvec(b1[:D], KCD))
            putv(("bpw1b", i), _colvec(b1[D:], KCD))
            wd = inp["c_w_dw"][ci]
            wv = np.zeros((128, CW * KCD), np.float32)
            for c in range(KCD):
                wv[:, c * CW:(c + 1) * CW] = wd[:, c * 128:(c + 1) * 128].T
            putv(("wdw", i), wv)
            putv(("bdw", i), _colvec(inp["c_b_dw"][ci], KCD))
            putv(("lng", i), _colvec(inp["c_ln_g"][ci], KCD))
            putv(("lnb", i), _colvec(inp["c_ln_b"][ci], KCD))
            W2 = inp["c_w_pw2"][ci]
            for oc in range(KCD):
                putw(("p2", i, oc), _blk(W2, allk, oc * 128 + np.arange(128)))
            putv(("bpw2", i), _colvec(inp["c_b_pw2"][ci], KCD))
            ci += 1
        Wgu, Wd = inp["f_w_gate_up"][i], inp["f_w_down"][i]
        for g in range(P.NGRP):
            for j in range(P.FG):
                m = g * P.FG + j
                putw(("fg", i, g, j), _blk(Wgu, allk, m * 128 + np.arange(128)))
                putw(("fu", i, g, j), _blk(Wgu, allk, P.DFF + m * 128 + np.arange(128)))
            kcs = list(range(g * P.FG, (g + 1) * P.FG))
            for oc in range(KCD):
                putw(("fd", i, g, oc), _blk(Wd, kcs, oc * 128 + np.arange(128)))
    putv("nfin", _colvec(inp["norm_final"], KCD))
    return wall, vecs


def prep_core(P, inp, c, shared):
    D, KCD, OWN, TP = P.D, P.KCD, P.OWN, P.TP
    f32 = np.float32
    xp = inp["x_prompt"][0]
    lo = c * OWN - HALO
    seg = np.zeros((TP, D), f32)
    a = max(lo, 0)
    seg[a - lo:] = xp[a:(c + 1) * OWN]
    xs = inp["x_sample"][c * P.SPC:(c + 1) * P.SPC].reshape(P.NSC, D)
    allx = np.concatenate([seg, xs], axis=0)
    xin = np.ascontiguousarray(allx.T.reshape(KCD, 128, TP + P.NSC).transpose(1, 0, 2))
    pos = np.concatenate([lo + np.arange(TP), cfgpast(P) + np.tile(np.arange(P.DS), P.SPC)]).astype(f32)
    inv = (np.float32(ROPE_THETA) ** (-np.arange(0, ROT, 2, dtype=f32) / np.float32(ROT))).astype(f32)
    ang = pos[:, None] * inv[None, :]
    cs, sn = np.cos(ang).astype(f32), np.sin(ang).astype(f32)
    rope = np.zeros((128, 2, TP + P.NSC), f32)
    rope[:, 0, :] = 1.0
    for hh in range(2):
        for d in range(8):
            rope[hh * 64 + d, 0] = cs[:, d]
            rope[hh * 64 + d, 1] = -sn[:, d]
            rope[hh * 64 + 8 + d, 0] = cs[:, d]
            rope[hh * 64 + 8 + d, 1] = sn[:, d]
    valid = (lo + np.arange(TP)) >= 0
    kb = np.full((64, P.NCH + 2), 0.0, f32)
    kb[:, :2] = NEG
    kb[:, 2:] = np.where(valid, 0.0, NEG).astype(f32).reshape(P.NCH, 64).T
    tokmask = np.ascontiguousarray(np.broadcast_to(valid.astype(f32)[None, :], (128, TP)))
    pcorr = np.ones((128, 4, PPRE), f32)
    if c == 0:
        for g, w in enumerate(POOL_W):
            for j in range(PPRE):
                pcorr[:, g, j] = f32(w) / f32(min(j + 1, w))
    NA = max(P.NA, 1)
    vb = np.zeros((NA, 64, P.NKV * 64), f32)
    sinkrow = np.zeros((NA, 1, P.NH * 64), f32)
    nq = P.NH * 64
    for j in range(P.NA):
        vb[j] = inp["a_b_qkv"][j][None, nq + P.NKV * 64:]
        s = inp["a_sinks"][j].reshape(P.NKV, 4, 2)
        sinkrow[j, 0] = np.repeat(s.transpose(0, 2, 1).reshape(-1), 64)
    sl = slice(c * P.SPC, (c + 1) * P.SPC)
    ck = inp["cache_k"][:NA, sl]
    cv = inp["cache_v"][:NA, sl]
    Wc = ck.shape[2]
    assert Wc == 128
    kcT = np.ascontiguousarray(np.concatenate([ck.transpose(0, 4, 1, 3, 2)] * 2, axis=1))
    cvb = cv.reshape(NA, P.SPC, 2, 64, P.NKV, 64).transpose(0, 3, 1, 2, 4, 5)
    vc = np.ascontiguousarray(np.stack([cvb, cvb], axis=5).reshape(NA, 64, P.SPC, 2, P.NKV, 128))
    kc_raw = np.ascontiguousarray(ck.reshape(NA, P.SPC, Wc, P.NKV * 64))
    vc_raw = np.ascontiguousarray(cv.reshape(NA, P.SPC, Wc, P.NKV * 64))
    sp = inp["state_pool"][0, sl]
    spool = np.ascontiguousarray(sp.transpose(2, 0, 1).reshape(KCD, 128, P.SPC, PPRE).transpose(1, 0, 2, 3))
    sc_ = inp["state_conv"][0, sl]
    sconv = np.ascontiguousarray(sc_.transpose(2, 0, 1).reshape(KCD, 128, P.SPC, CPRE).transpose(1, 0, 2, 3))
    wall, vecs = shared
    rperm = np.zeros((128, 128), f32)
    pm = _rope_perm()
    for hh in range(2):
        for d in range(ROT):
            rperm[hh * 64 + pm[d], hh * 64 + d] = 1.0
    return dict(xin=xin, wall=wall, vecs=vecs, rope=rope, kbias=kb, tokmask=tokmask, pcorr=pcorr, vb=vb, ident=np.eye(128, dtype=f32),
                rperm=rperm,
                sinkrow=sinkrow, kcT=kcT, vc=vc, kc_raw=kc_raw, vc_raw=vc_raw, spool=spool, sconv=sconv)


def cfgpast(P):
    return P.cfg["PAST"]


def build(P):
    nc = bass.Bass("TRN2", target_bir_lowering=False)
    D, KCD, NKV, NQC, FG = P.D, P.KCD, P.NKV, P.NQC, P.FG
    TP, NSC, OWN, NTT, NA, SPC, DS = P.TP, P.NSC, P.OWN, P.NTT, max(P.NA, 1), P.SPC, P.DS
    KVW = NKV * 64
    MAXCH = max(P.TILES)

    def din(name, shape):
        return nc.dram_tensor(name, list(shape), F32, kind="ExternalInput").ap()

    def dout(name, shape):
        return nc.dram_tensor(name, list(shape), F32, kind="ExternalOutput").ap()
    xin = din("xin", [128, KCD, TP + NSC])
    wall = din("wall", [128, P.WTOT])
    vecs_d = din("vecs", [128, P.NV])
    rope_d = din("rope", [128, 2, TP + NSC])
    kbias_d = din("kbias", [64, P.NCH + 2])
    tokmask_d = din("tokmask", [128, TP])
    pcorr_d = din("pcorr", [128, 4, PPRE])
    ident_d = din("ident", [128, 128])
    rperm_d = din("rperm", [128, 128])
    vb_d = din("vb", [NA, 64, KVW])
    sinkrow_d = din("sinkrow", [NA, 1, P.NH * 64])
    kcT_d = din("kcT", [NA, 128, SPC, NKV, 128])
    vc_d = din("vc", [NA, 64, SPC, 2, NKV, 128])
    kc_raw = din("kc_raw", [NA, SPC, 128, KVW])
    vc_raw = din("vc_raw", [NA, SPC, 128, KVW])
    spool_d = din("spool", [128, KCD, SPC, PPRE])
    sconv_d = din("sconv", [128, KCD, SPC, CPRE])
    yout = dout("yout", [128, KCD, OWN + NSC])
    kp_out = dout("kp_out", [NA, 64, NKV, 128])
    vp_out = dout("vp_out", [NA, 64, 2, KVW])
    ks_old = dout("ks_old", [NA, SPC, 128 - DS, KVW])
    vs_old = dout("vs_old", [NA, SPC, 128 - DS, KVW])
    ks_new = dout("ks_new", [NA, 64, NKV, NSC])
    vs_new = dout("vs_new", [NA, DS, SPC, KVW])
    pp_out = dout("pp_out", [128, KCD, PPRE])
    ps_out = dout("ps_out", [128, KCD, SPC, PPRE])
    cp_out = dout("cp_out", [128, KCD, CPRE])
    cs_out = dout("cs_out", [128, KCD, SPC, CPRE])

    es = contextlib.ExitStack()
    with es:
        S = Sched(nc, es)
        PE, ACT, DVE, POOL, SP = S.PE, S.ACT, S.DVE, S.POOL, S.SP

        def sb(name, shape, dt=F32):
            return es.enter_context(nc.sbuf_tensor("sb_" + name, list(shape), dt))
        x_t = sb("x", [128, KCD, NTT])
        hA = sb("hA", [128, KCD, NTT], BF16)
        BW = max(KCD, 2 * FG) * NTT
        Bs = sb("Bs", [128, BW], BF16)
        QT = Bs[:, 0:KCD * NTT].rearrange("p (k n) -> p k n", k=KCD)
        MID = Bs[:, 0:2 * FG * NTT].rearrange("p (b j n) -> p b j n", b=2, j=FG)
        KT = sb("KT", [128, 2, NKV, 128 + NTT], BF16)
        Vd = sb("Vd", [64, MAXCH + 2, NKV, 128], BF16)
        KTst = [sb(f"KTst{j}", [128, 2, NKV, 128], BF16) for j in range(NA)]
        Vst = [sb(f"Vst{j}", [64, 2, NKV, 128], BF16) for j in range(NA)]
        NPT = 2
        PT = [sb(f"PT{i}", [64, 3, 512], BF16) for i in range(NPT)]
        ropeT = sb("ropeT", [128, 2, NTT])
        vecs = sb("vecs", [128, P.NV])
        kbias = sb("kbias", [64, P.NCH + 2])
        WS = [sb(f"ws{i}", [128, P.SLOTW], BF16) for i in range(P.NSLOT)]
        wsem = [S.newsem(f"s_w{i}") for i in range(P.NSLOT)]
        NTMP = 5
        TMP = [sb(f"tmp{i}", [128, 512]) for i in range(NTMP)]
        NST = 2 if NTT > 512 else 1
        RSTD = [sb(f"rstd{i}", [128, 512]) for i in range(NST)]
        MU = [sb(f"mu{i}", [128, 512]) for i in range(NST)]
        SEQW = max(CPRE + MAXCH * CHUNK, SPC * (CPRE + DS))
        NSEQ = 8
        SEQ = [sb(f"seq{i}", [128, SEQW]) for i in range(NSEQ)]
        NDG = 4
        DG = [sb(f"dg{i}", [128, 128]) for i in range(NDG)]
        ident_f = sb("ident_f", [128, 128])
        rperm_f = sb("rperm_f", [128, 128])
        s1acc = sb("s1acc", [128, NTT])
        s2acc = sb("s2acc", [128, NTT])
        pstate = sb("pstate", [128, KCD, PPRE])
        cstate = sb("cstate", [128, KCD, CPRE])
        ES_all = sb("ES_all", [128, KCD, SPC, PPRE + DS])
        US_all = sb("US_all", [128, KCD, SPC, CPRE + DS])
        tokm = sb("tokm", [128, P.TILES[0] * CHUNK])
        pcorr = sb("pcorr", [128, 4, PPRE])
        vb = sb("vbt", [64, KVW])
        esrow = sb("esrow", [1, P.NH * 64], BF16)
        ones_bf = sb("ones_bf", [128, 128], BF16)
        ones_f = sb("ones_f", [128, 128])
        KTc = sb("KTc", [128, 2, SPC, NKV, 128], BF16)
        Vc = sb("Vc", [64, SPC, 2, NKV, 128], BF16)
        Vs = sb("Vs", [DS, SPC, NKV, 128], BF16)
        kf32 = sb("kf32", [64, NKV, 128 + NSC])
        RC = sb("rcol", [64, MAXCH + SPC])
        vf32 = sb("vf32", [64, 2 + SPC, KVW])
        PS = [es.enter_context(nc.psum_tensor(f"ps{i}", [128, 512], F32)) for i in range(8)]

        class NS:
            pass
        B = NS()
        B.x = [[Buf() for _ in range(2)] for _ in range(KCD)]
        B.h = [[Buf() for _ in range(2)] for _ in range(KCD)]
        B.q = [[Buf() for _ in range(2)] for _ in range(KCD)]
        B.mid = [[[Buf() for _ in range(2)] for _ in range(FG)] for _ in range(2)]
        B.bs_all = Buf()
        B.kt = Buf(); B.vd = Buf(); B.ktst = [Buf() for _ in range(NA)]; B.vst = [Buf() for _ in range(NA)]
        B.pt = [[Buf() for _ in range(3)] for _ in range(NPT)]
        B.rope = Buf(); B.vecs = Buf(); B.kbias = Buf()
        B.ws = [Buf() for _ in range(P.NSLOT)]
        B.tmp = [Buf() for _ in range(NTMP)]
        B.rstd = [Buf() for _ in range(2)]; B.mu = [Buf() for _ in range(2)]
        B.seq = [Buf() for _ in range(NSEQ)]
        B.dg = [Buf() for _ in range(NDG)]
        B.ident = Buf()
        B.rc = Buf()
        B.rperm = Buf()
        B.s1 = Buf(); B.s2 = Buf(); B.pstate = [Buf() for _ in range(KCD)]; B.cstate = [Buf() for _ in range(KCD)]
        B.es = [Buf() for _ in range(KCD)]; B.us = [Buf() for _ in range(KCD)]
        B.tokm = Buf(); B.pcorr = Buf(); B.vb = Buf(); B.sinkf = Buf(); B.esrow = Buf(); B.ones = Buf()
        B.ktc = Buf(); B.vc = Buf(); B.vs = Buf(); B.kf32 = Buf(); B.vf32 = Buf()
        B.yt = [Buf() for _ in range(2)]
        B.ps = [Buf() for _ in range(8)]
        B.dram = Buf()
        ctr = dict(ps=0, ws=0, tmp=0, seq=0, pt=0, yt=0, dg=0)

        ps_reserved = set()

        def rot(kind, n):
            while True:
                i = ctr[kind] % n
                ctr[kind] += 1
                if kind != "ps" or i not in ps_reserved:
                    return i

        def vcol(name, i=0):
            o = P.vcol[name] + i
            return vecs[:, o:o + 1]

        def wload(name):
            o, n = P.wblk[name]
            s = rot("ws", P.NSLOT)
            S.dma(POOL, lambda e, s=s, o=o, n=n: e.dma_start(out=WS[s][:, 0:n], in_=wall[:, o:o + n]),
                  writes=[B.ws[s]], sc=wsem[s])
            return s

        def linear(units, KCb, act, act_bufs, subt, evac, fine_first=False, hook=None, hook_at=0):
            deferred = []
            PF = max(2, P.NSLOT // max(len(u) for u in units) - 1)
            pending = [[wload(nm) for nm in unit] for unit in units[:PF]]
            for ui, unit in enumerate(units):
                if ui + PF < len(units):
                    pending.append([wload(nm) for nm in units[ui + PF]])
                slots = pending[ui]
                for st, (c0, n) in enumerate(subt):
                    banks = []
                    for bi_, s in enumerate(slots):
                        b = rot("ps", 8)
                        banks.append(b)

                        if fine_first and ui == 0:
                            ab = act_bufs(st)
                            for kc in range(KCb):
                                S.op(PE, lambda e, s=s, b=b, c0=c0, n=n, kc=kc: e.matmul(
                                    PS[b][:, 0:n], WS[s][:, kc * 128:(kc + 1) * 128], act(kc, c0, n),
                                    start=(kc == 0), stop=(kc == KCb - 1)),
                                    reads=[B.ws[s], ab[kc]], writes=[B.ps[b]], skip_own=(kc > 0))
                            continue

                        def mm(e, s=s, b=b, c0=c0, n=n, st=st):
                            ins = None
                            for kc in range(KCb):
                                ins = e.matmul(PS[b][:, 0:n], WS[s][:, kc * 128:(kc + 1) * 128], act(kc, c0, n),
                                               start=(kc == 0), stop=(kc == KCb - 1))
                            return ins
                        S.op(PE, mm, reads=[B.ws[s]] + act_bufs(st), writes=[B.ps[b]])
                    if hook is not None and ui <= hook_at:
                        deferred.append((ui, st, c0, n, banks))
                        if ui == min(hook_at, len(units) - 1) and st == len(subt) - 1:
                            hook()
                            for d_ in deferred:
                                evac(*d_)
                            hook = None
                        continue
                    evac(ui, st, c0, n, banks)

        def tmp():
            i = rot("tmp", NTMP)
            return i

        def rsqrt_inplace(st, n):
            S.op(ACT, lambda e: e.sqrt(out=RSTD[st][:, 0:n], in_=RSTD[st][:, 0:n]), reads=[B.rstd[st]], writes=[B.rstd[st]])
            S.op(DVE, lambda e: e.reciprocal(out=RSTD[st][:, 0:n], in_=RSTD[st][:, 0:n]), reads=[B.rstd[st]], writes=[B.rstd[st]])

        def norm_stats(subt, mask_tile0=False, part=None):
            for st, (c0, n) in enumerate(subt):
                if part == "tail":
                    norm_stats_tail(st, c0, n, mask_tile0)
                    continue
                for kc in range(KCD):
                    ACC, accb = (MU[st], B.mu[st]) if kc % 2 == 0 else (RSTD[st], B.rstd[st])
                    if kc < 2:
                        S.op(ACT, lambda e, c0=c0, n=n, ACC=ACC, kc=kc: e.activation(out=ACC[:, 0:n], in_=x_t[:, kc, c0:c0 + n], func=AF.Square),
                             reads=[B.x[kc][st]], writes=[accb])
                        continue
                    t = tmp()
                    S.op(ACT, lambda e, t=t, kc=kc, c0=c0, n=n: e.activation(out=TMP[t][:, 0:n], in_=x_t[:, kc, c0:c0 + n],
                                                                            func=AF.Square),
                         reads=[B.x[kc][st]], writes=[B.tmp[t]])
                    S.op(DVE, lambda e, t=t, n=n, ACC=ACC: e.tensor_tensor(out=ACC[:, 0:n], in0=ACC[:, 0:n], in1=TMP[t][:, 0:n], op=ALU.add),
                         reads=[B.tmp[t], accb], writes=[accb])
                if KCD > 1:
                    S.op(DVE, lambda e, n=n, st=st: e.tensor_tensor(out=MU[st][:, 0:n], in0=MU[st][:, 0:n], in1=RSTD[st][:, 0:n], op=ALU.add),
                         reads=[B.mu[st], B.rstd[st]], writes=[B.mu[st]])
                if part != "chain":
                    norm_stats_tail(st, c0, n, mask_tile0)

        def norm_stats_tail(st, c0, n, mask_tile0):
            if True:
                b = rot("ps", 8)
                S.op(PE, lambda e, b=b, n=n, st=st: e.matmul(PS[b][:, 0:n], ones_f[:, :], MU[st][:, 0:n], start=True, stop=True),
                     reads=[B.mu[st], B.ones], writes=[B.ps[b]])
                S.op(DVE, lambda e, b=b, st=st, n=n: e.tensor_scalar(out=RSTD[st][:, 0:n], in0=PS[b][:, 0:n],
                                                                    scalar1=1.0 / D, scalar2=EPS, op0=ALU.mult, op1=ALU.add),
                     reads=[B.ps[b]], writes=[B.rstd[st]])
                rsqrt_inplace(st, n)
                if mask_tile0:
                    nm = min(n, max(0, P.TILES[0] * CHUNK - c0))
                    if nm > 0:
                        S.op(DVE, lambda e, st=st, c0=c0, nm=nm: e.tensor_tensor(out=RSTD[st][:, 0:nm], in0=RSTD[st][:, 0:nm],
                                                                              in1=tokm[:, c0:c0 + nm], op=ALU.mult),
                             reads=[B.rstd[st], B.tokm], writes=[B.rstd[st]])

        def norm_to_h(gname, subt):
            norm_stats(subt)
            for st, (c0, n) in enumerate(subt):
                for kc in range(KCD):
                    S.op(DVE, lambda e, kc=kc, st=st, c0=c0, n=n: e.scalar_tensor_tensor(
                        out=hA[:, kc, c0:c0 + n], in0=x_t[:, kc, c0:c0 + n], scalar=vcol(gname, kc),
                        in1=RSTD[st][:, 0:n], op0=ALU.mult, op1=ALU.mult),
                        reads=[B.x[kc][st], B.rstd[st], B.vecs], writes=[B.h[kc][st]])

        hbufs = lambda st: [B.h[kc][st] for kc in range(KCD)]
        hact = lambda kc, c0, n: hA[:, kc, c0:c0 + n]

        def layer_A(li, ti, NP, ncols, subt, has_s, last, subt_out=None, attn_from=0):
            j = sum(1 for k in P.LAYERS[:li] if k == "A")
            DBG = P.cfg.get("ADBG", ())
            if "nos" in DBG:
                has_s = False
            nch = NP // CHUNK
            ch0 = sum(P.TILES[:ti])
            S.dma(SP, lambda e: e.dma_start(out=vb[:, :], in_=vb_d[j]), writes=[B.vb])
            for h_ in range(NKV):
                ts_ = tmp()
                S.dma(SP, lambda e, h_=h_, ts_=ts_: e.dma_start(out=TMP[ts_][0:1, 0:512], in_=sinkrow_d[j][:, h_ * 512:(h_ + 1) * 512]),
                      writes=[B.tmp[ts_]])
                S.op(ACT, lambda e, h_=h_, ts_=ts_: e.activation(out=esrow[0:1, h_ * 512:(h_ + 1) * 512], in_=TMP[ts_][0:1, 0:512],
                                                               func=AF.Exp),
                     reads=[B.tmp[ts_]], writes=[B.esrow])
            for z in range(2):
                S.op(ACT, lambda e, z=z: e.activation(out=KT[:, z, :, 0:128], in_=KTst[j][:, z, :, :], func=AF.Copy),
                     reads=[B.ktst[j]], writes=[B.kt])
            S.op(ACT, lambda e: e.activation(out=Vd[:, 0:2], in_=Vst[j][:, :], func=AF.Copy),
                 reads=[B.vst[j]], writes=[B.vd])
            if has_s:
                for z in range(2):
                    S.dma(POOL, lambda e, z=z: e.dma_start(out=KTc[z * 64:(z + 1) * 64, z], in_=kcT_d[j, z * 64:(z + 1) * 64]),
                          writes=[B.ktc])
                S.dma(POOL, lambda e: e.dma_start(out=Vc[:], in_=vc_d[j]), writes=[B.vc])
                if "nodd" not in DBG:
                    S.dma(SP, lambda e: e.dma_start(out=ks_old[j], in_=kc_raw[j, :, DS:128, :]), reads=[B.dram], writes=[])
                    S.dma(SP, lambda e: e.dma_start(out=vs_old[j], in_=vc_raw[j, :, DS:128, :]), reads=[B.dram], writes=[])
            for st, (c0, n) in enumerate(subt):
                for kc in range(KCD):
                    S.op(ACT, lambda e, kc=kc, c0=c0, n=n: e.mul(out=hA[:, kc, c0:c0 + n], in_=x_t[:, kc, c0:c0 + n],
                                                                mul=vcol(("nmix", li), kc)),
                         reads=[B.x[kc][st], B.vecs], writes=[B.h[kc][st]])
            norm_stats(subt, part="chain")

            RB, RBb = [s1acc, s2acc], [B.s1, B.s2]
            pend = {}

            def rope_finish(ui):
                isq = ui < NQC
                oc = ui if isq else ui - NQC
                r = ui % 2
                for (st, c0, n) in pend.pop(ui):
                    br = rot("ps", 8)
                    S.op(PE, lambda e, br=br, c0=c0, n=n: e.matmul(PS[br][:, 0:n], rperm_f[:, :], RB[r][:, c0:c0 + n], start=True, stop=True),
                         reads=[RBb[r], B.rperm], writes=[B.ps[br]])
                    t1, t2 = tmp(), tmp()
                    S.op(DVE, lambda e, br=br, c0=c0, n=n, t2=t2: e.tensor_tensor(out=TMP[t2][:, 0:n], in0=PS[br][:, 0:n],
                                                                                  in1=ropeT[:, 1, c0:c0 + n], op=ALU.mult),
                         reads=[B.ps[br], B.rope], writes=[B.tmp[t2]])
                    S.op(DVE, lambda e, c0=c0, n=n, t1=t1: e.tensor_tensor(out=TMP[t1][:, 0:n], in0=RB[r][:, c0:c0 + n],
                                                                          in1=ropeT[:, 0, c0:c0 + n], op=ALU.mult),
                         reads=[RBb[r], B.rope], writes=[B.tmp[t1]])
                    rope_out(isq, oc, st, c0, n, t1, t2)

            def evac_qk(ui, st, c0, n, banks):
                isq = ui < NQC
                oc = ui if isq else ui - NQC
                if st == 0 and ui > 0:
                    rope_finish(ui - 1)
                b0 = vcol(("bq", li) if isq else ("bk", li), oc)
                S.op(DVE, lambda e: e.tensor_tensor(out=RB[ui % 2][:, c0:c0 + n], in0=PS[banks[0]][:, 0:n], in1=RSTD[st][:, 0:n],
                                                    op=ALU.mult),
                     reads=[B.ps[banks[0]], B.rstd[st]], writes=[RBb[ui % 2]])
                S.op(ACT, lambda e: e.activation(out=RB[ui % 2][:, c0:c0 + n], in_=RB[ui % 2][:, c0:c0 + n], func=AF.Identity, bias=b0),
                     reads=[RBb[ui % 2], B.vecs], writes=[RBb[ui % 2]])
                pend.setdefault(ui, []).append((st, c0, n))

            def rope_out(isq, oc, st, c0, n, t1, t2):
                if isq:
                    S.op(DVE, lambda e: e.tensor_tensor(out=QT[:, oc, c0:c0 + n], in0=TMP[t1][:, 0:n], in1=TMP[t2][:, 0:n],
                                                        op=ALU.add),
                         reads=[B.tmp[t1], B.tmp[t2]], writes=[B.q[oc][st]])
                else:
                    S.op(DVE, lambda e: e.tensor_tensor(out=TMP[t1][:, 0:n], in0=TMP[t1][:, 0:n], in1=TMP[t2][:, 0:n],
                                                        op=ALU.add),
                         reads=[B.tmp[t1], B.tmp[t2]], writes=[B.tmp[t1]])
                    for z in range(2):
                        S.op(ACT, lambda e, z=z: e.activation(out=KT[z * 64:(z + 1) * 64, z, oc, 128 + c0:128 + c0 + n],
                                                             in_=TMP[t1][z * 64:(z + 1) * 64, 0:n], func=AF.Copy),
                             reads=[B.tmp[t1]], writes=[B.kt])
                    if last:
                        a, bnd = max(c0, NP - 128), min(c0 + n, NP)
                        if bnd > a:
                            S.op(ACT, lambda e, a=a, bnd=bnd: e.activation(out=kf32[:, oc, a - (NP - 128):bnd - (NP - 128)],
                                                             in_=TMP[t1][0:64, a - c0:bnd - c0], func=AF.Copy),
                                 reads=[B.tmp[t1]], writes=[B.kf32])
                    if has_s:
                        a, bnd = max(c0, NP), min(c0 + n, NP + NSC)
                        if bnd > a:
                            S.op(ACT, lambda e, a=a, bnd=bnd: e.activation(out=kf32[:, oc, 128 + a - NP:128 + bnd - NP],
                                                             in_=TMP[t1][0:64, a - c0:bnd - c0], func=AF.Copy),
                                 reads=[B.tmp[t1]], writes=[B.kf32])
            subt_out = subt_out or subt
            in_from = subt[0][0]
            units = [[("q", li, oc)] for oc in range(NQC)] + [[("k", li, h)] for h in range(NKV)]
            linear(units, KCD, hact, hbufs, subt, evac_qk, fine_first=True,
                   hook=lambda: norm_stats(subt, part="tail"), hook_at=2)
            rope_finish(len(units) - 1)

            vs0, vs1 = wload(("v", li, 0)), wload(("v", li, 1))
            HK = KCD // 2
            vgroups = [(c * CHUNK, CHUNK, ("p", c)) for c in range(in_from // CHUNK, nch)]
            if has_s:
                vgroups += [(NP + s * DS, DS, ("s", s)) for s in range(SPC)]
            bcol = rot("ps", 8)
            for gi, (c0, m, kind) in enumerate(vgroups):
                stv = [st for st, (a0, n) in enumerate(subt) if a0 <= c0 < a0 + n][0]
                a0 = subt[stv][0]
                S.op(PE, lambda e, gi=gi, c0=c0, m=m, stv=stv, a0=a0: e.matmul(
                    PS[bcol][0:m, gi:gi + 1], MU[stv][:, c0 - a0:c0 - a0 + m], ones_f[:, 0:1], start=True, stop=True),
                    reads=[B.mu[stv], B.ones], writes=[B.ps[bcol]], skip_own=(gi > 0))
            npg = sum(1 for g_ in vgroups if g_[2][0] == "p")
            for (r0, r1, q0_, q1_) in [(0, 64, 0, npg)] + ([(0, DS, npg, len(vgroups))] if len(vgroups) > npg else []):
                S.op(DVE, lambda e, r1=r1, q0_=q0_, q1_=q1_: e.tensor_scalar(out=RC[0:r1, q0_:q1_], in0=PS[bcol][0:r1, q0_:q1_],
                                                                            scalar1=1.0 / D, scalar2=EPS, op0=ALU.mult, op1=ALU.add),
                     reads=[B.ps[bcol]], writes=[B.rc])
                S.op(ACT, lambda e, r1=r1, q0_=q0_, q1_=q1_: e.sqrt(out=RC[0:r1, q0_:q1_], in_=RC[0:r1, q0_:q1_]),
                     reads=[B.rc], writes=[B.rc])
                S.op(DVE, lambda e, r1=r1, q0_=q0_, q1_=q1_: e.reciprocal(out=RC[0:r1, q0_:q1_], in_=RC[0:r1, q0_:q1_]),
                     reads=[B.rc], writes=[B.rc])
            for gi, (c0, m, kind) in enumerate(vgroups):
                b = rot("ps", 8)

                def mmv(e, b=b, c0=c0, m=m):
                    ins = None
                    for kc in range(KCD):
                        sl_ = vs0 if kc < HK else vs1
                        kk = kc % HK
                        ins = e.matmul(PS[b][0:m, 0:KVW], hA[:, kc, c0:c0 + m], WS[sl_][:, kk * KVW:(kk + 1) * KVW],
                                       start=(kc == 0), stop=(kc == KCD - 1))
                    return ins
                sts = sorted(set(st for st, (a0, n) in enumerate(subt) if a0 < c0 + m and c0 < a0 + n))
                S.op(PE, mmv, reads=[B.ws[vs0], B.ws[vs1]] + [B.h[kc][st] for kc in range(KCD) for st in sts],
                     writes=[B.ps[b]])
                t = tmp()
                S.op(DVE, lambda e, b=b, t=t, m=m, gi=gi: e.scalar_tensor_tensor(out=TMP[t][0:m, 0:KVW], in0=PS[b][0:m, 0:KVW],
                                                                                scalar=RC[0:m, gi:gi + 1], in1=vb[0:m, :],
                                                                                op0=ALU.mult, op1=ALU.add),
                     reads=[B.ps[b], B.vb, B.rc], writes=[B.tmp[t]])
                src3 = lambda t=t, m=m: TMP[t][0:m, 0:KVW].rearrange("p (h d) -> p h d", h=NKV)
                if kind[0] == "p":
                    c = kind[1]
                    for e2 in range(2):
                        S.op(ACT, lambda e, c=c, e2=e2, src3=src3: e.activation(out=Vd[:, 2 + c, :, e2 * 64:(e2 + 1) * 64],
                                                                                in_=src3(), func=AF.Copy),
                             reads=[B.tmp[t]], writes=[B.vd])
                    if last and c >= nch - 2:
                        S.op(ACT, lambda e, c=c, t=t: e.activation(out=vf32[:, c - (nch - 2), :], in_=TMP[t][0:64, 0:KVW],
                                                                   func=AF.Copy),
                             reads=[B.tmp[t]], writes=[B.vf32])
                else:
                    s = kind[1]
                    for e2 in range(2):
                        S.op(ACT, lambda e, s=s, e2=e2, src3=src3: e.activation(out=Vs[:, s, :, e2 * 64:(e2 + 1) * 64],
                                                                                in_=src3(), func=AF.Copy),
                             reads=[B.tmp[t]], writes=[B.vs])
                    S.op(ACT, lambda e, s=s, t=t: e.activation(out=vf32[0:DS, 2 + s, :], in_=TMP[t][0:DS, 0:KVW], func=AF.Copy),
                         reads=[B.tmp[t]], writes=[B.vf32])

            def attn_unit(h, nq, q0, keyblocks, qst):
                W4 = 4 * nq
                pt = rot("pt", NPT)
                sbanks = []
                for kb_i, (ktfn, vap, nk, bias, rb) in enumerate(keyblocks):
                    b = rot("ps", 8)
                    sbanks.append(b)

                    def mms(e, b=b, ktfn=ktfn, nk=nk):
                        ins = None
                        for e2 in P.cfg.get("E2S", (0, 1)):
                            ins = e.matmul(PS[b][0:nk, e2 * W4:(e2 + 1) * W4].rearrange("p (i q) -> p i q", i=4),
                                           ktfn(e2), QT[:, 4 * h:4 * h + 4, q0:q0 + nq],
                                           start=True, stop=True)
                        return ins
                    S.op(PE, mms, reads=rb + [B.q[4 * h + i][qst] for i in range(4)], writes=[B.ps[b]])
                ALVL = P.cfg.get("ALVL", 9)
                if ALVL < 2:
                    return
                for kb_i, (ktfn, vap, nk, bias, rb) in enumerate(keyblocks):
                    b = sbanks[kb_i]
                    if bias is None:
                        fn = lambda e, b=b, nk=nk, kb_i=kb_i: e.activation(out=PT[pt][0:nk, kb_i, 0:2 * W4],
                                                                           in_=PS[b][0:nk, 0:2 * W4], func=AF.Exp, scale=0.125)
                    else:
                        fn = lambda e, b=b, nk=nk, kb_i=kb_i, bias=bias: e.activation(
                            out=PT[pt][0:nk, kb_i, 0:2 * W4], in_=PS[b][0:nk, 0:2 * W4], func=AF.Exp, scale=0.125, bias=bias)
                    S.op(ACT, fn, reads=[B.ps[b], B.kbias], writes=[B.pt[pt][kb_i]])
                if ALVL < 3:
                    return
                bo, bd = rot("ps", 8), rot("ps", 8)
                nkb = len(keyblocks)

                def mmo(e):
                    ins = None
                    for kb_i, (ktfn, vap, nk, bias, rb) in enumerate(keyblocks):
                        ins = e.matmul(PS[bo][:, 0:2 * W4], vap, PT[pt][0:nk, kb_i, 0:2 * W4],
                                       start=(kb_i == 0), stop=(kb_i == nkb - 1))
                    return ins
                S.op(PE, mmo, reads=B.pt[pt][0:nkb] + [B.vd, B.vc, B.vs], writes=[B.ps[bo]])

                if ALVL < 4:
                    return

                def mmd(e):
                    for kb_i, (ktfn, vap, nk, bias, rb) in enumerate(keyblocks):
                        e.matmul(PS[bd][:, 0:2 * W4], ones_bf[0:nk, :], PT[pt][0:nk, kb_i, 0:2 * W4],
                                 start=(kb_i == 0), stop=False)
                    er = esrow[0:1, h * 512:(h + 1) * 512].rearrange("p (g q) -> p g q", g=8)[:, :, 0:nq]
                    return e.matmul(PS[bd][:, 0:2 * W4].rearrange("p (g q) -> p g q", g=8), ones_bf[0:1, :], er,
                                    start=False, stop=True)
                S.op(PE, mmd, reads=B.pt[pt][0:nkb] + [B.ones, B.esrow], writes=[B.ps[bd]])
                if ALVL < 5:
                    return
                t = tmp()
                for e2 in range(2 if ALVL >= 6 else 1):
                    rs, cs_ = slice(e2 * 64, (e2 + 1) * 64), slice(e2 * W4, (e2 + 1) * W4)
                    S.op(DVE, lambda e, rs=rs, cs_=cs_: e.reciprocal(out=TMP[t][rs, cs_], in_=PS[bd][rs, cs_]),
                         reads=[B.ps[bd]], writes=[B.tmp[t]])
                    S.op(DVE, lambda e, rs=rs, cs_=cs_: e.tensor_tensor(
                        out=hA[rs, 4 * h:4 * h + 4, q0:q0 + nq],
                        in0=PS[bo][rs, cs_].rearrange("p (i q) -> p i q", i=4),
                        in1=TMP[t][rs, cs_].rearrange("p (i q) -> p i q", i=4), op=ALU.mult),
                        reads=[B.ps[bo], B.tmp[t]], writes=[B.h[4 * h + i][qst] for i in range(4)])

            def st_of(col):
                for st, (a0, n) in enumerate(subt):
                    if a0 <= col < a0 + n:
                        return st
                raise AssertionError
            for n_ in range(attn_from // CHUNK, nch if "noattn" not in DBG else 0):
                q0 = n_ * CHUNK
                for h in range(NKV):
                    kbs = []
                    for d_ in range(3):
                        cc = n_ + d_
                        kbs.append((lambda e2, cc=cc, h=h: KT[:, e2, h, cc * 64:(cc + 1) * 64],
                                    Vd[:, cc, h, :], 64, kbias[:, ch0 + cc:ch0 + cc + 1], [B.kt]))
                    attn_unit(h, CHUNK, q0, kbs, st_of(q0))
            if has_s and "nosattn" not in DBG:
                for s in range(SPC):
                    q0 = NP + s * DS
                    for h in range(NKV):
                        kbs = []
                        for blk in range(2):
                            kbs.append((lambda e2, s=s, h=h, blk=blk: KTc[:, e2, s, h, blk * 64:(blk + 1) * 64],
                                        Vc[:, s, blk, h, :], 64, None, [B.ktc]))
                        kbs.append((lambda e2, q0=q0, h=h: KT[:, e2, h, 128 + q0:128 + q0 + DS],
                                    Vs[:, s, h, :], DS, None, [B.kt]))
                        attn_unit(h, DS, q0, kbs, st_of(q0))
            for z in range(2):
                S.op(ACT, lambda e, z=z: e.activation(out=KTst[j][:, z, :, :], in_=KT[:, z, :, NP:NP + 128], func=AF.Copy),
                     reads=[B.kt], writes=[B.ktst[j]])
            S.op(ACT, lambda e: e.activation(out=Vst[j][:, :], in_=Vd[:, nch:nch + 2], func=AF.Copy),
                 reads=[B.vd], writes=[B.vst[j]])
            if last:
                S.dma(SP, lambda e: e.dma_start(out=kp_out[j], in_=kf32[:, :, 0:128]), reads=[B.kf32])
                S.dma(SP, lambda e: e.dma_start(out=vp_out[j], in_=vf32[:, 0:2, :]), reads=[B.vf32])
            if has_s:
                S.dma(SP, lambda e: e.dma_start(out=ks_new[j], in_=kf32[:, :, 128:128 + NSC]), reads=[B.kf32])
                S.dma(SP, lambda e: e.dma_start(out=vs_new[j], in_=vf32[0:DS, 2:2 + SPC, :]), reads=[B.vf32])

            def evac_o(ui, st, c0, n, banks):
                S.op(DVE, lambda e: e.scalar_tensor_tensor(out=x_t[:, ui, c0:c0 + n], in0=PS[banks[0]][:, 0:n],
                                                           scalar=vcol(("bo", li), ui), in1=x_t[:, ui, c0:c0 + n],
                                                           op0=ALU.add, op1=ALU.add),
                     reads=[B.ps[banks[0]], B.x[ui][st], B.vecs], writes=[B.x[ui][st]])
            linear([[("o", li, oc)] for oc in range(KCD)], KCD, hact, hbufs, subt_out, evac_o, fine_first=True)

        def seq_views(buf, nseg, L):
            return SEQ[buf][:, 0:nseg * L].rearrange("p (s l) -> p s l", s=nseg)

        def layer_B(li, ti, NP, ncols, subt, has_s, last):
            norm_stats(subt, mask_tile0=(ti == 0))
            if has_s:
                for kc in range(KCD):
                    pass
                S.dma(SP, lambda e: e.dma_start(out=ES_all[:, :, :, 0:PPRE], in_=spool_d[:, :, :, :]), writes=B.es)
            for kc in range(KCD):
                g = kc // P.PG
                w = POOL_W[g]
                segs = [("p", 1, NP, 0)]
                if has_s:
                    segs.append(("s", SPC, DS, NP))
                for (kind, nseg, n, col0) in segs:
                    L = PPRE + n
                    if kind == "p":
                        e_i = rot("seq", NSEQ)
                        E = seq_views(e_i, 1, L)
                        ebuf = B.seq[e_i]
                        S.op(ACT, lambda e, E=E, kc=kc: e.activation(out=E[:, 0, 0:PPRE], in_=pstate[:, kc, :], func=AF.Copy),
                             reads=[B.pstate[kc]], writes=[ebuf])
                    else:
                        E = ES_all[:, kc]
                        ebuf = B.es[kc]
                    for st, (a0, nn) in enumerate(subt):
                        lo_, hi_ = max(a0, col0), min(a0 + nn, col0 + nseg * n)
                        if hi_ <= lo_:
                            continue
                        if kind == "p":
                            o_ap = E[:, 0, PPRE + lo_ - col0:PPRE + hi_ - col0]
                            i_ap = x_t[:, kc, lo_:hi_]
                            r_ap = RSTD[st][:, lo_ - a0:hi_ - a0]
                        else:
                            assert lo_ == col0 and hi_ == col0 + nseg * n
                            o_ap = E[:, :, PPRE:PPRE + n]
                            i_ap = x_t[:, kc, lo_:hi_].rearrange("p (s t) -> p s t", s=nseg)
                            r_ap = RSTD[st][:, lo_ - a0:hi_ - a0].rearrange("p (s t) -> p s t", s=nseg)
                        S.op(DVE, lambda e, o_ap=o_ap, i_ap=i_ap, r_ap=r_ap, kc=kc: e.scalar_tensor_tensor(
                            out=o_ap, in0=i_ap, scalar=vcol(("nmix", li), kc), in1=r_ap, op0=ALU.mult, op1=ALU.mult),
                            reads=[B.x[kc][st], B.rstd[st], B.vecs], writes=[ebuf])
                    cur, curb = E, ebuf
                    sh = 1
                    while sh < w:
                        o_i = rot("seq", NSEQ)
                        O = seq_views(o_i, nseg, L)
                        S.op(DVE, lambda e, O=O, cur=cur, sh=sh, L=L: e.tensor_tensor(
                            out=O[:, :, 2 * sh - 1:L], in0=cur[:, :, 2 * sh - 1:L], in1=cur[:, :, sh - 1:L - sh], op=ALU.add),
                            reads=[curb], writes=[B.seq[o_i]])
                        if sh > 1:
                            pass
                        cur, curb = O, B.seq[o_i]
                        sh *= 2
                    if kind == "p" and ti == 0:
                        S.op(DVE, lambda e, cur=cur, g=g: e.tensor_tensor(
                            out=cur[:, 0, PPRE + HALO:PPRE + HALO + PPRE], in0=cur[:, 0, PPRE + HALO:PPRE + HALO + PPRE],
                            in1=pcorr[:, g, :], op=ALU.mult), reads=[curb, B.pcorr], writes=[curb])
                    if kind == "p":
                        o_ap = hA[:, kc, 0:NP].rearrange("p (s t) -> p s t", s=1)
                        hb = [B.h[kc][st] for st in range(len(subt))]
                    else:
                        o_ap = hA[:, kc, col0:col0 + nseg * n].rearrange("p (s t) -> p s t", s=nseg)
                        hb = [B.h[kc][st_] for st_ in range(len(subt))]
                    S.op(DVE, lambda e, o_ap=o_ap, cur=cur, E=E, n=n, w=w: e.scalar_tensor_tensor(
                        out=o_ap, in0=cur[:, :, PPRE:PPRE + n], scalar=1.0 / w, in1=E[:, :, PPRE:PPRE + n],
                        op0=ALU.mult, op1=ALU.subtract), reads=[curb, ebuf], writes=hb)
                    if kind == "p":
                        S.op(ACT, lambda e, E=E, kc=kc, n=n: e.activation(out=pstate[:, kc, :], in_=E[:, 0, n:n + PPRE], func=AF.Copy),
                             reads=[ebuf], writes=[B.pstate[kc]])
            if last:
                S.dma(SP, lambda e: e.dma_start(out=pp_out[:, :, :], in_=pstate[:, :, :]), reads=B.pstate)
            if has_s:
                S.dma(SP, lambda e: e.dma_start(out=ps_out[:, :, :, :], in_=ES_all[:, :, :, DS:DS + PPRE]), reads=B.es)

            def evac_p(ui, st, c0, n, banks):
                oc = ui
                S.op(DVE, lambda e: e.scalar_tensor_tensor(out=x_t[:, oc, c0:c0 + n], in0=PS[banks[0]][:, 0:n],
                                                           scalar=vcol(("bscale", li), oc), in1=x_t[:, oc, c0:c0 + n],
                                                           op0=ALU.mult, op1=ALU.add),
                     reads=[B.ps[banks[0]], B.x[oc][st], B.vecs], writes=[B.x[oc][st]])
            for g in range(4):
                units = [[("pg", li, g, oc)] for oc in range(P.PG)]
                linear(units, P.PG, lambda kc, c0, n, g=g: hA[:, g * P.PG + kc, c0:c0 + n],
                       lambda st, g=g: [B.h[g * P.PG + kc][st] for kc in range(P.PG)], subt,
                       lambda ui, st, c0, n, banks, g=g: evac_p(g * P.PG + ui, st, c0, n, banks), fine_first=True)

        def layer_C(li, ti, NP, ncols, subt, has_s, last):
            norm_to_h(("nmix", li), subt)
            if has_s:
                S.dma(SP, lambda e: e.dma_start(out=US_all[:, :, :, 0:CPRE], in_=sconv_d[:, :, :, :]), writes=B.us)
            state = {}

            def evac_u(ui, st, c0, n, banks):
                c = ui
                if st == 0:
                    u_i = rot("seq", NSEQ)
                    state["u"] = u_i
                    U = seq_views(u_i, 1, CPRE + NP)
                    S.op(ACT, lambda e: e.activation(out=U[:, 0, 0:CPRE], in_=cstate[:, c, :], func=AF.Copy),
                         reads=[B.cstate[c]], writes=[B.seq[u_i]])
                u_i = state["u"]
                U = seq_views(u_i, 1, CPRE + NP)
                t = tmp()
                S.op(ACT, lambda e: e.activation(out=TMP[t][:, 0:n], in_=PS[banks[1]][:, 0:n], func=AF.Sigmoid,
                                                 bias=vcol(("bpw1b", li), c)),
                     reads=[B.ps[banks[1]], B.vecs], writes=[B.tmp[t]])
                lo_, hi_ = c0, min(c0 + n, NP)
                if hi_ > lo_:
                    S.op(DVE, lambda e: e.scalar_tensor_tensor(out=U[:, 0, CPRE + lo_:CPRE + hi_], in0=PS[banks[0]][:, 0:hi_ - lo_],
                                                               scalar=vcol(("bpw1a", li), c), in1=TMP[t][:, 0:hi_ - lo_],
                                                               op0=ALU.add, op1=ALU.mult),
                         reads=[B.ps[banks[0]], B.tmp[t], B.vecs], writes=[B.seq[u_i]])
                    if ti == 0:
                        S.op(DVE, lambda e: e.tensor_tensor(out=U[:, 0, CPRE + lo_:CPRE + hi_], in0=U[:, 0, CPRE + lo_:CPRE + hi_],
                                                            in1=tokm[:, lo_:hi_], op=ALU.mult),
                             reads=[B.seq[u_i], B.tokm], writes=[B.seq[u_i]])
                if has_s and c0 + n > NP:
                    assert c0 <= NP and c0 + n == NP + NSC
                    o_ = NP - c0
                    S.op(DVE, lambda e: e.scalar_tensor_tensor(
                        out=US_all[:, c, :, CPRE:CPRE + DS],
                        in0=PS[banks[0]][:, o_:o_ + NSC].rearrange("p (s t) -> p s t", s=SPC),
                        scalar=vcol(("bpw1a", li), c),
                        in1=TMP[t][:, o_:o_ + NSC].rearrange("p (s t) -> p s t", s=SPC), op0=ALU.add, op1=ALU.mult),
                        reads=[B.ps[banks[0]], B.tmp[t], B.vecs], writes=[B.us[c]])
                if st != len(subt) - 1:
                    return
                segs = [(U, B.seq[u_i], 1, NP, 0)]
                if has_s:
                    segs.append((US_all[:, c], B.us[c], SPC, DS, NP))
                for si_, (X, xb, nseg, n_, col0) in enumerate(segs):
                    npe = P.cfg.get("CONV_NPE", 10) if si_ == 0 else 0
                    nd = CW - npe
                    NDA = 4 if si_ == 0 else 2
                    a_i = [rot("seq", NSEQ) for _ in range(NDA)]
                    A = [seq_views(a, nseg, n_) for a in a_i]
                    bc = None
                    if npe > 0:
                        bc = rot("ps", 8)
                        for jt in range(nd, CW):
                            d_ = rot("dg", NDG)
                            wc = vcol(("wdw", li), c * CW + jt)
                            S.op(ACT, lambda e, d_=d_, wc=wc: e.mul(out=DG[d_][:, :], in_=ident_f[:, :], mul=wc),
                                 reads=[B.ident, B.vecs], writes=[B.dg[d_]])
                            S.op(PE, lambda e, d_=d_, jt=jt, bc=bc, n_=n_, nd=nd: e.matmul(
                                PS[bc][:, 0:n_], DG[d_][:, :], SEQ[u_i][:, jt:jt + n_], start=(jt == nd), stop=(jt == CW - 1)),
                                reads=[B.dg[d_], xb], writes=[B.ps[bc]])
                    for jt in range(nd):
                        k_ = jt % NDA
                        wc = vcol(("wdw", li), c * CW + jt)
                        if jt < NDA:
                            S.op(DVE, lambda e, A=A, k_=k_, X=X, jt=jt, wc=wc, n_=n_: e.tensor_scalar(
                                out=A[k_][:, :, :], in0=X[:, :, jt:jt + n_], scalar1=wc, scalar2=None, op0=ALU.mult),
                                reads=[xb, B.vecs], writes=[B.seq[a_i[k_]]])
                        else:
                            S.op(DVE, lambda e, A=A, k_=k_, X=X, jt=jt, wc=wc, n_=n_: e.scalar_tensor_tensor(
                                out=A[k_][:, :, :], in0=X[:, :, jt:jt + n_], scalar=wc, in1=A[k_][:, :, :],
                                op0=ALU.mult, op1=ALU.add),
                                reads=[xb, B.vecs, B.seq[a_i[k_]]], writes=[B.seq[a_i[k_]]])
                    S.op(DVE, lambda e, A=A: e.scalar_tensor_tensor(out=A[0][:, :, :], in0=A[0][:, :, :],
                                                                    scalar=vcol(("bdw", li), c), in1=A[1][:, :, :],
                                                                    op0=ALU.add, op1=ALU.add),
                         reads=[B.seq[a_i[0]], B.seq[a_i[1]], B.vecs], writes=[B.seq[a_i[0]]])
                    if NDA == 4:
                        S.op(DVE, lambda e, A=A: e.tensor_tensor(out=A[2][:, :, :], in0=A[2][:, :, :], in1=A[3][:, :, :], op=ALU.add),
                             reads=[B.seq[a_i[2]], B.seq[a_i[3]]], writes=[B.seq[a_i[2]]])
                        S.op(DVE, lambda e, A=A: e.tensor_tensor(out=A[0][:, :, :], in0=A[0][:, :, :], in1=A[2][:, :, :], op=ALU.add),
                             reads=[B.seq[a_i[0]], B.seq[a_i[2]]], writes=[B.seq[a_i[0]]])
                    if bc is not None:
                        S.op(DVE, lambda e, A=A, bc=bc, n_=n_: e.tensor_tensor(
                            out=A[0][:, :, :], in0=PS[bc][:, 0:n_].rearrange("p (s t) -> p s t", s=1), in1=A[0][:, :, :], op=ALU.add),
                            reads=[B.seq[a_i[0]], B.ps[bc]], writes=[B.seq[a_i[0]]])
                    if si_ == 0:
                        S.op(ACT, lambda e: e.activation(out=cstate[:, c, :], in_=U[:, 0, NP:NP + CPRE], func=AF.Copy),
                             reads=[B.seq[u_i]], writes=[B.cstate[c]])
                    cv3 = lambda ap2, nseg=nseg: ap2.rearrange("p (s t) -> p s t", s=nseg)
                    qb = [B.q[c][st_] for st_ in range(len(subt))]
                    S.op(ACT, lambda e, A=A, cv3=cv3, col0=col0, nseg=nseg, n_=n_: e.activation(
                        out=cv3(QT[:, c, col0:col0 + nseg * n_]), in_=A[0][:, :, :], func=AF.Copy),
                        reads=[B.seq[a_i[0]]], writes=qb)
                    S.op(ACT, lambda e, A=A: e.activation(out=A[1][:, :, :], in_=A[0][:, :, :], func=AF.Square),
                         reads=[B.seq[a_i[0]]], writes=[B.seq[a_i[1]]])
                    if c == 0:
                        S.op(DVE, lambda e, A=A, cv3=cv3, col0=col0, nseg=nseg, n_=n_: e.tensor_copy(
                            out=cv3(s1acc[:, col0:col0 + nseg * n_]), in_=A[0][:, :, :]), reads=[B.seq[a_i[0]]], writes=[B.s1])
                        S.op(DVE, lambda e, A=A, cv3=cv3, col0=col0, nseg=nseg, n_=n_: e.tensor_copy(
                            out=cv3(s2acc[:, col0:col0 + nseg * n_]), in_=A[1][:, :, :]), reads=[B.seq[a_i[1]]], writes=[B.s2])
                    else:
                        S.op(DVE, lambda e, A=A, cv3=cv3, col0=col0, nseg=nseg, n_=n_: e.tensor_tensor(
                            out=cv3(s1acc[:, col0:col0 + nseg * n_]), in0=cv3(s1acc[:, col0:col0 + nseg * n_]), in1=A[0][:, :, :],
                            op=ALU.add), reads=[B.seq[a_i[0]], B.s1], writes=[B.s1])
                        S.op(DVE, lambda e, A=A, cv3=cv3, col0=col0, nseg=nseg, n_=n_: e.tensor_tensor(
                            out=cv3(s2acc[:, col0:col0 + nseg * n_]), in0=cv3(s2acc[:, col0:col0 + nseg * n_]), in1=A[1][:, :, :],
                            op=ALU.add), reads=[B.seq[a_i[1]], B.s2], writes=[B.s2])
            units = [[("pa", li, c), ("pb", li, c)] for c in range(KCD)]
            linear(units, KCD, hact, hbufs, subt, evac_u, fine_first=True)
            if last:
                S.dma(SP, lambda e: e.dma_start(out=cp_out[:, :, :], in_=cstate[:, :, :]), reads=B.cstate)
            if has_s:
                S.dma(SP, lambda e: e.dma_start(out=cs_out[:, :, :, :], in_=US_all[:, :, :, DS:DS + CPRE]), reads=B.us)
            for st, (c0, n) in enumerate(subt):
                b1, b2 = rot("ps", 8), rot("ps", 8)
                S.op(PE, lambda e, b1=b1, c0=c0, n=n: e.matmul(PS[b1][:, c0:c0 + n], ones_f[:, :], s1acc[:, c0:c0 + n], start=True, stop=True),
                     reads=[B.s1, B.ones], writes=[B.ps[b1]])
                S.op(PE, lambda e, b2=b2, c0=c0, n=n: e.matmul(PS[b2][:, c0:c0 + n], ones_f[:, :], s2acc[:, c0:c0 + n], start=True, stop=True),
                     reads=[B.s2, B.ones], writes=[B.ps[b2]])
                S.op(DVE, lambda e, b1=b1, st=st, n=n, c0=c0: e.tensor_scalar(out=MU[st][:, 0:n], in0=PS[b1][:, c0:c0 + n], scalar1=1.0 / D,
                                                                      scalar2=None, op0=ALU.mult),
                     reads=[B.ps[b1]], writes=[B.mu[st]])
                t = tmp()
                S.op(DVE, lambda e, st=st, n=n, t=t: e.tensor_tensor(out=TMP[t][:, 0:n], in0=MU[st][:, 0:n], in1=MU[st][:, 0:n], op=ALU.mult),
                     reads=[B.mu[st]], writes=[B.tmp[t]])
                S.op(DVE, lambda e, b2=b2, st=st, n=n, t=t, c0=c0: e.scalar_tensor_tensor(out=RSTD[st][:, 0:n], in0=PS[b2][:, c0:c0 + n], scalar=1.0 / D,
                                                                                  in1=TMP[t][:, 0:n], op0=ALU.mult, op1=ALU.subtract),
                     reads=[B.ps[b2], B.tmp[t]], writes=[B.rstd[st]])
                S.op(DVE, lambda e, st=st, n=n: e.tensor_scalar(out=RSTD[st][:, 0:n], in0=RSTD[st][:, 0:n], scalar1=EPS, scalar2=None,
                                                               op0=ALU.add),
                     reads=[B.rstd[st]], writes=[B.rstd[st]])
                rsqrt_inplace(st, n)
                for c in range(KCD):
                    t = tmp()
                    S.op(DVE, lambda e, c=c, c0=c0, n=n, t=t, st=st: e.tensor_tensor(out=TMP[t][:, 0:n], in0=QT[:, c, c0:c0 + n],
                                                                                    in1=MU[st][:, 0:n], op=ALU.subtract),
                         reads=[B.q[c][st], B.mu[st]], writes=[B.tmp[t]])
                    S.op(DVE, lambda e, n=n, t=t, st=st: e.tensor_tensor(out=TMP[t][:, 0:n], in0=TMP[t][:, 0:n], in1=RSTD[st][:, 0:n],
                                                                        op=ALU.mult),
                         reads=[B.tmp[t], B.rstd[st]], writes=[B.tmp[t]])
                    S.op(ACT, lambda e, c=c, c0=c0, n=n, t=t: e.activation(out=hA[:, c, c0:c0 + n], in_=TMP[t][:, 0:n], func=AF.Silu,
                                                                          scale=vcol(("lng", li), c), bias=vcol(("lnb", li), c)),
                         reads=[B.tmp[t], B.vecs], writes=[B.h[c][st]])

            def evac_2(ui, st, c0, n, banks):
                S.op(DVE, lambda e: e.scalar_tensor_tensor(out=x_t[:, ui, c0:c0 + n], in0=PS[banks[0]][:, 0:n],
                                                           scalar=vcol(("bpw2", li), ui), in1=x_t[:, ui, c0:c0 + n],
                                                           op0=ALU.add, op1=ALU.add),
                     reads=[B.ps[banks[0]], B.x[ui][st], B.vecs], writes=[B.x[ui][st]])
            linear([[("p2", li, oc)] for oc in range(KCD)], KCD, hact, hbufs, subt, evac_2, fine_first=True)

        def ffn(li, subt):
            for st, (c0, n) in enumerate(subt):
                for kc in range(KCD):
                    S.op(ACT, lambda e, kc=kc, c0=c0, n=n: e.mul(out=hA[:, kc, c0:c0 + n], in_=x_t[:, kc, c0:c0 + n],
                                                                mul=vcol(("nffn", li), kc)),
                         reads=[B.x[kc][st], B.vecs], writes=[B.h[kc][st]])
            norm_stats(subt, part="chain")
            def gu(g):
                mb = g % 2

                def evac_gu(ui, st, c0, n, banks, mb=mb):
                    t = tmp()
                    S.op(DVE, lambda e: e.tensor_tensor(out=TMP[t][:, 0:n], in0=PS[banks[0]][:, 0:n], in1=RSTD[st][:, 0:n], op=ALU.mult),
                         reads=[B.ps[banks[0]], B.rstd[st]], writes=[B.tmp[t]])
                    S.op(ACT, lambda e: e.activation(out=TMP[t][:, 0:n], in_=TMP[t][:, 0:n], func=AF.Silu),
                         reads=[B.tmp[t]], writes=[B.tmp[t]])
                    S.op(DVE, lambda e: e.tensor_tensor(out=TMP[t][:, 0:n], in0=TMP[t][:, 0:n], in1=RSTD[st][:, 0:n], op=ALU.mult),
                         reads=[B.tmp[t], B.rstd[st]], writes=[B.tmp[t]])
                    S.op(DVE, lambda e: e.tensor_tensor(out=MID[:, mb, ui, c0:c0 + n], in0=PS[banks[1]][:, 0:n],
                                                        in1=TMP[t][:, 0:n], op=ALU.mult),
                         reads=[B.ps[banks[1]], B.tmp[t]], writes=[B.mid[mb][ui][st]])
                units = [[("fg", li, g, j), ("fu", li, g, j)] for j in range(FG)]
                linear(units, KCD, hact, hbufs, subt, evac_gu, fine_first=(g == 0),
                       hook=(lambda: norm_stats(subt, part="tail")) if g == 0 else None, hook_at=2)

            def dn(g):
                mb = g % 2

                def evac_d(ui, st, c0, n, banks):
                    S.op(DVE, lambda e: e.tensor_tensor(out=x_t[:, ui, c0:c0 + n], in0=PS[banks[0]][:, 0:n],
                                                        in1=x_t[:, ui, c0:c0 + n], op=ALU.add),
                         reads=[B.ps[banks[0]], B.x[ui][st]], writes=[B.x[ui][st]])
                linear([[("fd", li, g, oc)] for oc in range(KCD)], FG,
                       lambda kc, c0, n, mb=mb: MID[:, mb, kc, c0:c0 + n],
                       lambda st, mb=mb: [B.mid[mb][kc][st] for kc in range(FG)], subt, evac_d)

            gu(0)
            for g in range(P.NGRP):
                if g + 1 < P.NGRP:
                    gu(g + 1)
                dn(g)

        def alias_guard():
            allb = [b for row in B.q for b in row] + [b for m in B.mid for row in m for b in row]
            S.op(DVE, lambda e: e.memset(TMP[0][0:1, 0:1], 0.0), reads=allb, writes=allb + [B.tmp[0]])

        S.op(DVE, lambda e: e.memset(ones_bf[:, :], 1.0), writes=[B.ones])
        S.op(DVE, lambda e: e.memset(ones_f[:, :], 1.0), writes=[B.ones])
        for j in range(NA):
            for z in range(2):
                S.op(DVE, lambda e, j=j, z=z: e.memset(KTst[j][:, z, :, :], 0.0), writes=[B.ktst[j]])
            S.op(DVE, lambda e, j=j: e.memset(Vst[j][:, :], 0.0), writes=[B.vst[j]])
        for z in range(2):
            S.op(DVE, lambda e, z=z: e.memset(KT[:, z, :, :], 0.0), writes=[B.kt])
            for s_ in range(SPC):
                S.op(DVE, lambda e, z=z, s_=s_: e.memset(KTc[:, z, s_, :, :], 0.0), writes=[B.ktc])
        for i_ in range(NSEQ):
            S.op(DVE, lambda e, i_=i_: e.memset(SEQ[i_][:, :], 0.0), writes=[B.seq[i_]])
        S.op(DVE, lambda e: e.memset(pstate[:, :, :], 0.0), writes=B.pstate)
        S.op(DVE, lambda e: e.memset(cstate[:, :, :], 0.0), writes=B.cstate)
        S.dma(SP, lambda e: e.dma_start(out=vecs[:, :], in_=vecs_d[:, :]), writes=[B.vecs])
        S.dma(SP, lambda e: e.dma_start(out=kbias[:, :], in_=kbias_d[:, :]), writes=[B.kbias])
        S.dma(SP, lambda e: e.dma_start(out=tokm[:, :], in_=tokmask_d[:, 0:P.TILES[0] * CHUNK]), writes=[B.tokm])
        S.dma(SP, lambda e: e.dma_start(out=pcorr[:, :, :], in_=pcorr_d[:, :, :]), writes=[B.pcorr])
        S.dma(SP, lambda e: e.dma_start(out=ident_f[:, :], in_=ident_d[:, :]), writes=[B.ident])
        S.dma(SP, lambda e: e.dma_start(out=rperm_f[:, :], in_=rperm_d[:, :]), writes=[B.rperm])

        t0 = 0
        for ti, tch in enumerate(P.TILES):
            NP = tch * CHUNK
            has_s = (ti == P.STILE)
            last = (ti == len(P.TILES) - 1)
            ncols = NP + (NSC if has_s else 0)
            if ncols > 512:
                h1 = (ncols // 2 + 15) // 16 * 16
                subt = [(0, h1), (h1, ncols - h1)]
            else:
                subt = [(0, ncols)]
            for kc in range(KCD):
                xk = [B.x[kc][st] for st in range(2)]
                S.dma(SP, lambda e, t0=t0, NP=NP, kc=kc: e.dma_start(out=x_t[:, kc, 0:NP], in_=xin[:, kc, t0:t0 + NP]), writes=xk)
                if has_s:
                    S.dma(SP, lambda e, NP=NP, kc=kc: e.dma_start(out=x_t[:, kc, NP:NP + NSC], in_=xin[:, kc, TP:TP + NSC]), writes=xk)
            S.dma(SP, lambda e, t0=t0, NP=NP: e.dma_start(out=ropeT[:, :, 0:NP], in_=rope_d[:, :, t0:t0 + NP]), writes=[B.rope])
            if has_s:
                S.dma(SP, lambda e, NP=NP: e.dma_start(out=ropeT[:, :, NP:NP + NSC], in_=rope_d[:, :, TP:TP + NSC]), writes=[B.rope])
            def dump_x():
                for st, (c0, n) in enumerate(subt):
                    for kc in range(KCD):
                        own_lo = max(t0 + c0, HALO)
                        own_hi = min(t0 + c0 + n, t0 + NP)
                        if own_hi > own_lo:
                            S.dma(SP, lambda e, kc=kc, a=own_lo - t0, b=own_hi - t0, o=own_lo - HALO:
                                  e.dma_start(out=yout[:, kc, o:o + (b - a)], in_=x_t[:, kc, a:b]), reads=[B.x[kc][st]])
                        if has_s and c0 + n > NP:
                            a = max(c0, NP)
                            S.dma(SP, lambda e, kc=kc, a=a, b=c0 + n, o=OWN + a - NP:
                                  e.dma_start(out=yout[:, kc, o:o + (b - a)], in_=x_t[:, kc, a:b]), reads=[B.x[kc][st]])
            DUMP = P.cfg.get("DUMPX")
            trim = {}
            if ti == 0 and P.LAYERS == "ABCA" and len(subt) == 1 and not P.cfg.get("NOTRIM"):
                trim = {0: dict(mix=0, attn=128, out=128, ffn=128), 1: dict(mix=128, ffn=128),
                        2: dict(mix=128, ffn=192), 3: dict(mix=192, attn=HALO, out=HALO, ffn=HALO)}
            frm = lambda a: [(a, ncols - a)] if trim else subt
            for li, kind in enumerate(P.LAYERS):
                alias_guard()
                tr = trim.get(li, {})
                if kind == "A":
                    layer_A(li, ti, NP, ncols, frm(tr.get("mix", 0)), has_s, last,
                            subt_out=frm(tr.get("out", 0)), attn_from=tr.get("attn", 0))
                elif kind == "B":
                    layer_B(li, ti, NP, ncols, frm(tr.get("mix", 0)), has_s, last)
                else:
                    layer_C(li, ti, NP, ncols, frm(tr.get("mix", 0)), has_s, last)
                if DUMP == (li, "mix"):
                    dump_x()
                alias_guard()
                ffn(li, frm(tr.get("ffn", 0)))
                if DUMP == (li, "ffn"):
                    dump_x()
            norm_stats(subt)
            for st, (c0, n) in enumerate(subt if DUMP is None else []):
                for kc in range(KCD):
                    yi = tmp()
                    S.op(DVE, lambda e, kc=kc, st=st, c0=c0, n=n, yi=yi: e.scalar_tensor_tensor(
                        out=TMP[yi][:, 0:n], in0=x_t[:, kc, c0:c0 + n], scalar=vcol("nfin", kc),
                        in1=RSTD[st][:, 0:n], op0=ALU.mult, op1=ALU.mult),
                        reads=[B.x[kc][st], B.rstd[st], B.vecs], writes=[B.tmp[yi]])
                    own_lo = max(t0 + c0, HALO)
                    own_hi = min(t0 + c0 + n, t0 + NP)
                    if own_hi > own_lo:
                        S.dma(SP, lambda e, kc=kc, yi=yi, a=own_lo - t0 - c0, b=own_hi - t0 - c0, o=own_lo - HALO:
                              e.dma_start(out=yout[:, kc, o:o + (b - a)], in_=TMP[yi][:, a:b]), reads=[B.tmp[yi]])
                    if has_s and c0 + n > NP:
                        a = max(c0, NP)
                        S.dma(SP, lambda e, kc=kc, yi=yi, a=a - c0, b=n, o=OWN + a - NP:
                              e.dma_start(out=yout[:, kc, o:o + (b - a)], in_=TMP[yi][:, a:b]), reads=[B.tmp[yi]])
            t0 += NP

        with nc.Block() as block:
            @block.tensor
            def _(e):
                S.replay(PE, e)

            @block.scalar
            def _(e):
                S.replay(ACT, e)

            @block.vector
            def _(e):
                S.replay(DVE, e)

            @block.gpsimd
            def _(e):
                S.replay(POOL, e)

            @block.sync
            def _(e):
                S.replay(SP, e, final=True)
    return nc


def assemble(P, res):
    D, KCD, OWN, NA, SPC, DS, NKV = P.D, P.KCD, P.OWN, max(P.NA, 1), P.SPC, P.DS, P.NKV
    f32 = np.float32
    DB = P.cfg["DB"]
    y_p = np.empty((1, P.cfg["SEQ"], D), f32)
    y_s = np.empty((DB, DS, D), f32)
    ks = np.empty((NA, DB, 128, NKV, 64), f32)
    vs = np.empty((NA, DB, 128, NKV, 64), f32)
    ps = np.empty((1, DB, PPRE, D), f32)
    cs = np.empty((1, DB, CPRE, D), f32)

    def fm(a):
        a = np.moveaxis(a, (0, 1), (-1, -2))
        return a.reshape(a.shape[:-2] + (D,))
    for c in range(NCORES):
        r = res[c]
        y = fm(r["yout"])
        y_p[0, c * OWN:(c + 1) * OWN] = y[:OWN]
        y_s[c * SPC:(c + 1) * SPC] = y[OWN:].reshape(SPC, DS, D)
        sl = slice(c * SPC, (c + 1) * SPC)
        ks[:, sl, :128 - DS] = r["ks_old"].reshape(NA, SPC, 128 - DS, NKV, 64)
        vs[:, sl, :128 - DS] = r["vs_old"].reshape(NA, SPC, 128 - DS, NKV, 64)
        kn = r["ks_new"].reshape(NA, 64, NKV, SPC, DS)
        ks[:, sl, 128 - DS:] = kn.transpose(0, 3, 4, 2, 1)
        vn = r["vs_new"].reshape(NA, DS, SPC, NKV, 64)
        vs[:, sl, 128 - DS:] = vn.transpose(0, 2, 1, 3, 4)
        ps[0, sl] = fm(r["ps_out"])
        cs[0, sl] = fm(r["cs_out"])
    r = res[NCORES - 1]
    kp = r["kp_out"].transpose(0, 3, 2, 1)[:, None]
    vp = r["vp_out"].reshape(NA, 64, 2, NKV, 64).transpose(0, 2, 1, 3, 4).reshape(NA, 1, 128, NKV, 64)
    pp = fm(r["pp_out"])[None, None]
    cp = fm(r["cp_out"])[None, None]
    return (y_p, y_s, np.ascontiguousarray(kp), np.ascontiguousarray(vp), ks, vs,
            np.ascontiguousarray(pp), ps, np.ascontiguousarray(cp), cs)


def run(cfg, inputs, trace=False):
    P = Plan(cfg)
    inp = {k: np.asarray(v) for k, v in inputs.items()}
    shared = prep_shared(P, inp)
    in_maps = [prep_core(P, inp, c, shared) for c in range(NCORES)]
    nc = build(P)
    res = run_bass_kernel_spmd(nc, in_maps, core_ids=list(range(NCORES)), **({"trace": True} if trace else {}))
    return assemble(P, res.results), res


def kernel(**inputs):
    outs, _ = run(CFG_FULL, inputs)
    return outs
```
